# Optimizing a Trainium2 kernel written in Bass

```python
import math
import jax, jax.numpy as jnp
from jax import lax
import numpy as np

D_MODEL = 2048
BATCH = 1
SEQ = 8192
DEPTH = 2

D_MIX = D_MODEL
ATT_WIDTH = D_MIX // 2
HYENA_WIDTH = D_MIX - ATT_WIDTH
HEAD_DIM = 128
N_Q_HEADS = ATT_WIDTH // HEAD_DIM
N_KV_HEADS = 2
GQA_GROUP = N_Q_HEADS // N_KV_HEADS
KV_WIDTH = N_KV_HEADS * HEAD_DIM
HYENA_ORDER = 2
D_IN = ATT_WIDTH + 2 * KV_WIDTH + (HYENA_ORDER + 1) * HYENA_WIDTH
FILTER_EMB = 33
FILTER_HIDDEN = 64
DECAY_TARGET = 1e-2
FAST_DECAY_PCT = 0.3
SLOW_DECAY_PCT = 1.5
D_FF = 5504
GRID_W = 64
Q_BLOCK = 128
ROPE_THETA = 10000.0
ROW_DIMS = HEAD_DIM // 2
COL_DIMS = HEAD_DIM - ROW_DIMS
ALPHA = (2.0 * DEPTH) ** 0.25
BETA = (8.0 * DEPTH) ** -0.25
LN_EPS = 1e-5
RMS_EPS = 1e-6

kernel_name = "hybrid_attn_hyena_deepnorm_encoder"


def layer_norm(x, g, b):
    xf = x.astype(jnp.float32)
    mu = jnp.mean(xf, axis=-1, keepdims=True)
    var = jnp.mean(jnp.square(xf - mu), axis=-1, keepdims=True)
    return ((xf - mu) * lax.rsqrt(var + LN_EPS) * g + b).astype(x.dtype)


def rms_norm(x, g):
    xf = x.astype(jnp.float32)
    ms = jnp.mean(jnp.square(xf), axis=-1, keepdims=True)
    return (xf * lax.rsqrt(ms + RMS_EPS) * g).astype(x.dtype)


def dwconv3(x, w, b):
    xp = jnp.pad(x, ((0, 0), (1, 1), (0, 0)))
    return xp[:, :-2] * w[0] + xp[:, 1:-1] * w[1] + xp[:, 2:] * w[2] + b


def rope_tables(L):
    rows = L // GRID_W
    row_pos = jnp.repeat(jnp.arange(rows, dtype=jnp.float32), GRID_W)
    col_pos = jnp.tile(jnp.arange(GRID_W, dtype=jnp.float32), rows)

    def axis_table(pos, dims):
        inv = ROPE_THETA ** (-jnp.arange(0, dims, 2, dtype=jnp.float32) / dims)
        ang = pos[:, None] * inv[None, :]
        ang = jnp.concatenate([ang, ang], axis=-1)
        return jnp.cos(ang), jnp.sin(ang)

    cos_r, sin_r = axis_table(row_pos, ROW_DIMS)
    cos_c, sin_c = axis_table(col_pos, COL_DIMS)
    return cos_r, sin_r, cos_c, sin_c


def rotate_half(x):
    x1, x2 = jnp.split(x, 2, axis=-1)
    return jnp.concatenate([-x2, x1], axis=-1)


def apply_axial_rope(x, tables):
    cos_r, sin_r, cos_c, sin_c = tables
    xr, xc = x[..., :ROW_DIMS], x[..., ROW_DIMS:]
    xr = xr * cos_r[None, :, None, :] + rotate_half(xr) * sin_r[None, :, None, :]
    xc = xc * cos_c[None, :, None, :] + rotate_half(xc) * sin_c[None, :, None, :]
    return jnp.concatenate([xr, xc], axis=-1).astype(x.dtype)


def attention_group(q, k, v, q_g, k_g, tables):
    B, L, _ = q.shape
    q = apply_axial_rope(rms_norm(q.reshape(B, L, N_Q_HEADS, HEAD_DIM), q_g), tables)
    k = apply_axial_rope(rms_norm(k.reshape(B, L, N_KV_HEADS, HEAD_DIM), k_g), tables)
    v = v.reshape(B, L, N_KV_HEADS, HEAD_DIM)
    q = q.reshape(B, L, N_KV_HEADS, GQA_GROUP, HEAD_DIM).transpose(0, 2, 3, 1, 4)
    k = k.transpose(0, 2, 1, 3)
    v = v.transpose(0, 2, 1, 3)
    nb = L // Q_BLOCK
    qb = jnp.moveaxis(q.reshape(B, N_KV_HEADS, GQA_GROUP, nb, Q_BLOCK, HEAD_DIM), 3, 0)
    scale = HEAD_DIM ** -0.5

    def block(qi):
        s = jnp.einsum('bhgqd,bhkd->bhgqk', qi, k).astype(jnp.float32) * scale
        p = jax.nn.softmax(s, axis=-1)
        return jnp.einsum('bhgqk,bhkd->bhgqd', p.astype(v.dtype), v)

    o = lax.map(block, qb)
    return o.transpose(1, 0, 4, 2, 3, 5).reshape(B, L, ATT_WIDTH)


def hyena_filters(L, w1, b1, f1, w2, b2, f2, w3, b3, decay):
    bands = (FILTER_EMB - 1) // 2
    t = jnp.linspace(0.0, 1.0, L, dtype=jnp.float32)[:, None]
    w = 2.0 * math.pi * jnp.arange(L, dtype=jnp.float32)[:, None] / L
    f = jnp.linspace(1e-4, bands - 1, bands, dtype=jnp.float32)[None, :]
    z = jnp.concatenate([t, jnp.cos(f * w), -jnp.sin(f * w)], axis=-1)
    h = jnp.sin(f1 * (z @ w1 + b1))
    h = jnp.sin(f2 * (h @ w2 + b2))
    h = (h @ w3 + b3).astype(jnp.float32).reshape(L, 2, HYENA_WIDTH)
    h = h * jnp.exp(-t[:, :, None] * jnp.abs(decay).astype(jnp.float32)[None])
    h = h / jnp.sum(jnp.abs(h), axis=(0, 1), keepdims=True)
    return h[:, 0], h[:, 1]


def bidir_fftconv(z, h_fwd, h_bwd, skip):
    L = z.shape[1]
    C = z.shape[2]
    kern = jnp.concatenate([h_fwd, jnp.zeros((1, C), jnp.float32), h_bwd[1:][::-1]], axis=0)
    K = jnp.fft.rfft(kern, n=2 * L, axis=0)
    Z = jnp.fft.rfft(z.astype(jnp.float32), n=2 * L, axis=1)
    y = jnp.fft.irfft(Z * K[None], n=2 * L, axis=1)[:, :L]
    return (y + z.astype(jnp.float32) * skip).astype(z.dtype)


def hyena_group(u, conv_w, conv_b, w1, b1, f1, w2, b2, f2, w3, b3, decay, skip):
    u = dwconv3(u, conv_w, conv_b)
    x0, x1, v = jnp.split(u, 3, axis=-1)
    h_fwd, h_bwd = hyena_filters(u.shape[1], w1, b1, f1, w2, b2, f2, w3, b3, decay)
    return x0 * bidir_fftconv(x1 * v, h_fwd, h_bwd, skip)


def setup_inputs(seed: int = 0) -> dict:
    key = jax.random.key(seed)
    ks = iter(jax.random.split(key, 48))

    def nrm(shape, scale):
        return jax.random.normal(next(ks), shape, jnp.float32) * scale

    def gain(shape):
        return 1.0 + nrm(shape, 0.1)

    s_in = D_MODEL ** -0.5
    x = nrm((BATCH, SEQ, D_MODEL), 1.0)
    ln_in_g = gain((D_MODEL,))
    ln_in_b = nrm((D_MODEL,), 0.02)
    w_q = nrm((DEPTH, D_MODEL, ATT_WIDTH), s_in)
    w_k = nrm((DEPTH, D_MODEL, KV_WIDTH), s_in)
    w_v = nrm((DEPTH, D_MODEL, KV_WIDTH), s_in * BETA)
    w_gates = nrm((DEPTH, D_MODEL, 2 * HYENA_WIDTH), s_in)
    w_hv = nrm((DEPTH, D_MODEL, HYENA_WIDTH), s_in * BETA)
    w_in = jnp.concatenate([w_q, w_k, w_v, w_gates, w_hv], axis=-1)
    q_norm_g = gain((DEPTH, HEAD_DIM))
    k_norm_g = gain((DEPTH, HEAD_DIM))
    hy_conv_w = nrm((DEPTH, 3, 3 * HYENA_WIDTH), 3 ** -0.5)
    hy_conv_b = nrm((DEPTH, 3 * HYENA_WIDTH), 0.02)
    filt_w1 = nrm((DEPTH, FILTER_EMB, FILTER_HIDDEN), FILTER_EMB ** -0.5)
    filt_b1 = nrm((DEPTH, FILTER_HIDDEN), 0.02)
    filt_f1 = gain((DEPTH, FILTER_HIDDEN))
    filt_w2 = nrm((DEPTH, FILTER_HIDDEN, FILTER_HIDDEN), FILTER_HIDDEN ** -0.5)
    filt_b2 = nrm((DEPTH, FILTER_HIDDEN), 0.02)
    filt_f2 = gain((DEPTH, FILTER_HIDDEN))
    filt_w3 = nrm((DEPTH, FILTER_HIDDEN, 2 * HYENA_WIDTH), FILTER_HIDDEN ** -0.5)
    filt_b3 = nrm((DEPTH, 2 * HYENA_WIDTH), 0.02)
    min_decay = abs(math.log(DECAY_TARGET)) / SLOW_DECAY_PCT
    max_decay = abs(math.log(DECAY_TARGET)) / FAST_DECAY_PCT
    base_decay = jnp.linspace(min_decay, max_decay, HYENA_WIDTH, dtype=jnp.float32)
    filt_decay = base_decay[None, None, :] * (1.0 + nrm((DEPTH, 2, HYENA_WIDTH), 0.05))
    hy_skip = nrm((DEPTH, HYENA_WIDTH), 0.1)
    att_out_g = gain((DEPTH, ATT_WIDTH))
    hy_out_g = gain((DEPTH, HYENA_WIDTH))
    w_out = nrm((DEPTH, D_MIX, D_MODEL), D_MIX ** -0.5 * BETA)
    b_out = nrm((DEPTH, D_MODEL), 0.02)
    ln1_g = gain((DEPTH, D_MODEL))
    ln1_b = nrm((DEPTH, D_MODEL), 0.02)
    w_up = nrm((DEPTH, D_MODEL, 2 * D_FF), s_in)
    b_up = nrm((DEPTH, 2 * D_FF), 0.02)
    ffn_conv_w = nrm((DEPTH, 3, D_FF), 3 ** -0.5)
    ffn_conv_b = nrm((DEPTH, D_FF), 0.02)
    w_down = nrm((DEPTH, D_FF, D_MODEL), D_FF ** -0.5 * BETA)
    b_down = nrm((DEPTH, D_MODEL), 0.02)
    ln2_g = gain((DEPTH, D_MODEL))
    ln2_b = nrm((DEPTH, D_MODEL), 0.02)
    return {
        "x": x, "ln_in_g": ln_in_g, "ln_in_b": ln_in_b, "w_in": w_in,
        "q_norm_g": q_norm_g, "k_norm_g": k_norm_g,
        "hy_conv_w": hy_conv_w, "hy_conv_b": hy_conv_b,
        "filt_w1": filt_w1, "filt_b1": filt_b1, "filt_f1": filt_f1,
        "filt_w2": filt_w2, "filt_b2": filt_b2, "filt_f2": filt_f2,
        "filt_w3": filt_w3, "filt_b3": filt_b3, "filt_decay": filt_decay, "hy_skip": hy_skip,
        "att_out_g": att_out_g, "hy_out_g": hy_out_g, "w_out": w_out, "b_out": b_out,
        "ln1_g": ln1_g, "ln1_b": ln1_b, "w_up": w_up, "b_up": b_up,
        "ffn_conv_w": ffn_conv_w, "ffn_conv_b": ffn_conv_b, "w_down": w_down, "b_down": b_down,
        "ln2_g": ln2_g, "ln2_b": ln2_b,
    }


def reference(x, ln_in_g, ln_in_b, w_in, q_norm_g, k_norm_g, hy_conv_w, hy_conv_b,
              filt_w1, filt_b1, filt_f1, filt_w2, filt_b2, filt_f2, filt_w3, filt_b3,
              filt_decay, hy_skip, att_out_g, hy_out_g, w_out, b_out, ln1_g, ln1_b,
              w_up, b_up, ffn_conv_w, ffn_conv_b, w_down, b_down, ln2_g, ln2_b):
    L = x.shape[1]
    tables = rope_tables(L)
    x = layer_norm(x, ln_in_g, ln_in_b)
    c_q = ATT_WIDTH
    c_k = c_q + KV_WIDTH
    c_v = c_k + KV_WIDTH
    for l in range(DEPTH):
        proj = x @ w_in[l]
        q, k, v, hy = proj[..., :c_q], proj[..., c_q:c_k], proj[..., c_k:c_v], proj[..., c_v:]
        a = rms_norm(attention_group(q, k, v, q_norm_g[l], k_norm_g[l], tables), att_out_g[l])
        h = rms_norm(hyena_group(hy, hy_conv_w[l], hy_conv_b[l], filt_w1[l], filt_b1[l], filt_f1[l],
                                 filt_w2[l], filt_b2[l], filt_f2[l], filt_w3[l], filt_b3[l],
                                 filt_decay[l], hy_skip[l]), hy_out_g[l])
        mix = jnp.concatenate([a, h], axis=-1) @ w_out[l] + b_out[l]
        x = layer_norm(ALPHA * x + mix, ln1_g[l], ln1_b[l])
        up = x @ w_up[l] + b_up[l]
        gate, val = jnp.split(up, 2, axis=-1)
        gate = dwconv3(gate, ffn_conv_w[l], ffn_conv_b[l])
        ffn = (jax.nn.gelu(gate) * val) @ w_down[l] + b_down[l]
        x = layer_norm(ALPHA * x + ffn, ln2_g[l], ln2_b[l])
    return x
```

```python
import math
import numpy as np
from contextlib import ExitStack
import concourse.bass as bass
import concourse.mybir as mybir
from concourse.bass_utils import run_bass_kernel_spmd

F32 = mybir.dt.float32
F32R = mybir.dt.float32r
BF16 = mybir.dt.bfloat16
AF = mybir.ActivationFunctionType
ALU = mybir.AluOpType
AX = mybir.AxisListType

D = 2048
L = 8192
NC = 8
TPC = L // NC
HT = 512
DIN = 4608
DFF = 5504
NFF = DFF // 128
ALPHA = (2.0 * 2) ** 0.25
LN_EPS = 1e-5
RMS_EPS = 1e-6


class Op:
    __slots__ = ("eng", "fn", "deps", "idx", "dma", "sig", "slot", "val", "j")


class _Rec:
    def __init__(self):
        self.call = None

    def __getattr__(self, name):
        def f(*args, **kwargs):
            self.call = (name, args, kwargs)
            return None
        return f


class Prog:
    ENG = ["pe", "dve", "act", "pool", "sp"]

    def __init__(self, nc, stack, ndma_slots=8):
        self.nc = nc
        self.stack = stack
        self.streams = {e: [] for e in self.ENG}
        self.lastw = {}
        self.readers = {}
        self.dma_count = {e: 0 for e in self.ENG}
        self.R = ndma_slots
        self.live_dma = []

    def sb(self, name, shape, dtype=F32):
        if not hasattr(self, "_cache"):
            self._cache = {}
        if name in self._cache:
            return self._cache[name]
        t = self.stack.enter_context(self.nc.sbuf_tensor(name, list(shape), dtype))
        self._cache[name] = t
        return t

    def ps(self, name, shape, dtype=F32):
        return self.stack.enter_context(self.nc.psum_tensor(name, list(shape), dtype))

    def add(self, eng, fn, r=(), w=(), dma=False):
        op = Op()
        op.eng = eng
        if fn is not None:
            rec = _Rec()
            fn(rec)
            name, args, kwargs = rec.call
            fn = (lambda e, name=name, args=args, kwargs=kwargs: getattr(e, name)(*args, **kwargs))
        op.fn = fn
        op.dma = dma
        op.sig = False
        deps = {}

        def adddep(d, kind):
            if d is op:
                return
            if deps.get(d) != "raw":
                deps[d] = kind

        for k in r:
            lw = self.lastw.get(k)
            if lw is not None:
                adddep(lw, "raw")
        for k in w:
            lw = self.lastw.get(k)
            if lw is not None:
                adddep(lw, "waw")
            rd = self.readers.get(k)
            if rd:
                for d in rd[0].values():
                    adddep(d, "war")
                for d in rd[1]:
                    adddep(d, "war")
        for k in r:
            rd = self.readers.setdefault(k, ({}, []))
            if dma:
                rd[1].append(op)
            else:
                rd[0][eng] = op
        for k in w:
            self.lastw[k] = op
            self.readers[k] = ({}, [])
        final = []
        best = {}
        for d, kind in deps.items():
            if d.dma:
                final.append(d)
                continue
            if (not dma) and d.eng == eng:
                if eng == "pe" or kind != "raw":
                    continue
            b = best.get(d.eng)
            if b is None or d.idx > b.idx:
                best[d.eng] = d
        final.extend(best.values())
        for d in final:
            d.sig = True
        op.deps = final
        op.idx = len(self.streams[eng])
        if dma:
            op.j = self.dma_count[eng]
            self.dma_count[eng] += 1
            op.slot = op.j % self.R
            op.val = 16 * (op.j // self.R + 1)
            op.sig = True
            self.live_dma.append(op)
        self.streams[eng].append(op)
        return op

    def barrier(self):
        lasts = {e: (self.streams[e][-1] if self.streams[e] else None) for e in self.ENG}
        dmas = list(self.live_dma)
        for e in self.ENG:
            op = Op()
            op.eng = e
            op.fn = None
            op.dma = False
            op.sig = False
            deps = list(dmas)
            for e2, lo in lasts.items():
                if lo is None:
                    continue
                if lo.dma:
                    for cand in reversed(self.streams[e2]):
                        if not cand.dma and cand.fn is not None:
                            lo = cand
                            break
                    else:
                        continue
                if lo.fn is None:
                    for cand in reversed(self.streams[e2]):
                        if not cand.dma and cand.fn is not None:
                            lo = cand
                            break
                    else:
                        continue
                if e2 != e or True:
                    deps.append(lo)
            for d in deps:
                d.sig = True
            op.deps = deps
            op.idx = len(self.streams[e])
            self.streams[e].append(op)
        self.lastw = {}
        self.readers = {}
        self.live_dma = []

    def emit(self):
        nc = self.nc
        st = self.stack
        sem_e = {e: st.enter_context(nc.semaphore("s_" + e)) for e in self.ENG}
        dsem = {}
        for e in self.ENG:
            if self.dma_count[e]:
                dsem[e] = [st.enter_context(nc.semaphore("d_%s%d" % (e, i))) for i in range(self.R)]
        for e in self.ENG:
            c = 0
            for op in self.streams[e]:
                if not op.dma and op.fn is not None:
                    if op.sig:
                        c += 1
                        op.val = c
        block = st.enter_context(nc.Block())
        hooks = {"pe": block.tensor, "dve": block.vector, "act": block.scalar,
                 "pool": block.gpsimd, "sp": block.sync}

        def run(e):
            def body(eng):
                seen = {}
                for op in self.streams[e]:
                    waits = []
                    for d in op.deps:
                        if d.dma:
                            waits.append((dsem[d.eng][d.slot], d.val))
                        else:
                            waits.append((sem_e[d.eng], d.val))
                    if op.dma and op.j >= self.R:
                        waits.append((dsem[e][op.slot], op.val - 16))
                    for s, v in waits:
                        key = id(s)
                        if seen.get(key, 0) < v:
                            eng.wait_ge(s, v)
                            seen[key] = v
                    if op.fn is None:
                        continue
                    ins = op.fn(eng)
                    if op.dma:
                        ins.then_inc(dsem[e][op.slot], 16)
                    elif op.sig:
                        ins.then_inc(sem_e[e], 1)
            hooks[e](body)

        for e in self.ENG:
            if self.streams[e]:
                run(e)


class Ctx:
    pass


class Arena:
    def __init__(self, t, ncols):
        self.t = t
        self.n = ncols
        self.off = 0

    def reset(self):
        self.off = 0

    def get(self, shape):
        npart = shape[0]
        n = 1
        for d in shape[1:]:
            n *= d
        assert self.off + n <= self.n, ("arena overflow", self.off, n, self.n)
        ap = self.t[0:npart, self.off:self.off + n]
        self.off += n
        if len(shape) == 3:
            ap = ap.rearrange("p (a b) -> p a b", a=shape[1])
        elif len(shape) == 4:
            ap = ap.rearrange("p (a b c) -> p a b c", a=shape[1], b=shape[2])
        return ap


def new_ctx(name):
    nc = bass.Bass("TRN2", target_bir_lowering=False)
    C = Ctx()
    C.nc = nc
    C.stack = ExitStack()
    C.P = Prog(nc, C.stack)
    C.ins = {}
    C.outs = {}
    C.uid = 0
    return C


def din(C, name, shape, dtype=F32):
    t = C.nc.dram_tensor(name, list(shape), dtype, kind="ExternalInput").ap()
    C.ins[name] = t
    return t


def dout(C, name, shape, dtype=F32):
    t = C.nc.dram_tensor(name, list(shape), dtype, kind="ExternalOutput").ap()
    C.outs[name] = t
    return t


def setup_common(C):
    P = C.P
    C.psall = P.ps("psall", [128, 4096])
    C.psb = [C.psall[:, i * 512:(i + 1) * 512] for i in range(8)]
    C.outkeys = []


def load_const(C, name, dram_ap, shape, dtype=F32, q="sp", key=None):
    P = C.P
    t = P.sb(name, shape, dtype)
    P.add(q, lambda e: e.dma_start(out=t[:], in_=dram_ap), w=[key or name], dma=True)
    return t


def store(C, dram_ap, sb_ap, rkeys, q="sp"):
    P = C.P
    C.uid += 1
    k = "out%d" % C.uid
    C.outkeys.append(k)
    P.add(q, lambda e: e.dma_start(out=dram_ap, in_=sb_ap), r=rkeys, w=[k], dma=True)


def finish(C):
    P = C.P
    P.add("sp", None, r=C.outkeys)
    P.emit()
    C.stack.close()


def emit_ln(C, src, srckey, T, gt, bt, dst32, dstbf, dstkey, tag, alpha=None, add=None, addkey=None, pkeys=("lnp", "lnp2"), c0=0):
    P = C.P
    nb = (T + 511) // 512
    sq = [P.sb("ln_sqb%d" % (i,), [128, 512], BF16) for i in range(2)]
    m2 = P.sb("ln_m2", [128, 512])
    var = P.sb("ln_var", [128, 512])
    rstd = P.sb("ln_rstd", [128, 512])
    nmr = P.sb("ln_nmr", [128, 512])
    u = [P.sb("ln_u%d" % (i,), [128, 512]) for i in range(2)]
    tag = ""
    for b in range(nb):
        t0 = b * 512
        tw = min(512, T - t0)
        sl = slice(c0 + t0, c0 + t0 + tw)
        dl = slice(t0, t0 + tw) if c0 else sl
        pm = C.psb[6]
        pq = C.psb[7]
        for c in range(16):
            if add is not None:
                P.add("dve", lambda e, c=c, sl=sl: e.scalar_tensor_tensor(
                    out=src[:, c, sl], in0=src[:, c, sl], scalar=float(alpha), in1=add[:, c, sl],
                    op0=ALU.mult, op1=ALU.add), r=[(srckey, c), (addkey, c)], w=[(srckey, c)])
            s = sq[c % 2]
            P.add("act", lambda e, c=c, s=s, sl=sl, tw=tw: e.activation(out=s[:, 0:tw], in_=src[:, c, sl], func=AF.Square),
                  r=[(srckey, c)], w=["lnsq%s%d" % (tag, c % 2)])
            P.add("pe", lambda e, c=c, sl=sl, tw=tw: e.matmul(pm[:, 0:tw], lhsT=C.onesD[:], rhs=src[:, c, sl], start=(c == 0), stop=(c == 15)),
                  r=[(srckey, c), "onesD"], w=["psb6"])
            P.add("pe", lambda e, c=c, s=s, tw=tw: e.matmul(pq[:, 0:tw], lhsT=C.onesDb[:], rhs=s[:, 0:tw], start=(c == 0), stop=(c == 15)),
                  r=["lnsq%s%d" % (tag, c % 2), "onesD"], w=["psb7"])
        P.add("act", lambda e, tw=tw: e.activation(out=m2[:, 0:tw], in_=pm[:, 0:tw], func=AF.Square), r=["psb6"], w=["lnm2" + tag])
        P.add("dve", lambda e, tw=tw: e.tensor_tensor(out=var[:, 0:tw], in0=pq[:, 0:tw], in1=m2[:, 0:tw], op=ALU.subtract),
              r=["psb7", "lnm2" + tag], w=["lnvar" + tag])
        P.add("act", lambda e, tw=tw: e.activation(out=var[:, 0:tw], in_=var[:, 0:tw], func=AF.Sqrt, bias=C.eps_ln[:, 0:1]),
              r=["lnvar" + tag, "eps"], w=["lnvar" + tag])
        P.add("dve", lambda e, tw=tw: e.reciprocal(out=rstd[:, 0:tw], in_=var[:, 0:tw]), r=["lnvar" + tag], w=["lnrstd" + tag])
        P.add("dve", lambda e, tw=tw: e.scalar_tensor_tensor(out=nmr[:, 0:tw], in0=pm[:, 0:tw], scalar=-1.0, in1=rstd[:, 0:tw],
                                                              op0=ALU.mult, op1=ALU.mult),
              r=["psb6", "lnrstd" + tag], w=["lnnmr" + tag])
        for c in range(16):
            uu = u[c % 2]
            uk = "lnu%s%d" % (tag, c % 2)
            P.add("dve", lambda e, c=c, uu=uu, sl=sl, tw=tw: e.scalar_tensor_tensor(
                out=uu[:, 0:tw], in0=src[:, c, sl], scalar=gt[:, c:c + 1], in1=rstd[:, 0:tw], op0=ALU.mult, op1=ALU.mult),
                r=[(srckey, c), "lnrstd" + tag, pkeys[0]], w=[uk])
            P.add("dve", lambda e, c=c, uu=uu, tw=tw: e.scalar_tensor_tensor(
                out=uu[:, 0:tw], in0=nmr[:, 0:tw], scalar=gt[:, c:c + 1], in1=uu[:, 0:tw], op0=ALU.mult, op1=ALU.add),
                r=[uk, "lnnmr" + tag, pkeys[0]], w=[uk])
            if dst32 is not None:
                P.add("act", lambda e, c=c, uu=uu, dl=dl, tw=tw: e.activation(out=dst32[:, c, dl], in_=uu[:, 0:tw], func=AF.Identity, bias=bt[:, c:c + 1]),
                      r=[uk, pkeys[1]], w=[(dstkey + "32", c)])
            if dstbf is not None:
                P.add("act", lambda e, c=c, uu=uu, dl=dl, tw=tw: e.activation(out=dstbf[:, c, dl], in_=uu[:, 0:tw], func=AF.Identity, bias=bt[:, c:c + 1]),
                      r=[uk, pkeys[1]], w=[(dstkey + "bf", c)])


def emit_inproj(C, xb, xbkey, w_in, qg_t, kg_t, cos_t, sin_t, tcol0, o_q, o_k, o_v, o_hy, ocol0, NT=1):
    P = C.P
    wv = w_in.rearrange("(c p) n -> p c n", p=128)
    pend = []
    for og in range(9):
        wb = C.wbuf[og % len(C.wbuf)]
        wk = "wbuf%d" % (og % len(C.wbuf))
        P.add(getattr(C, "wq", "pool"), lambda e, og=og, wb=wb: e.dma_start(out=wb[:], in_=wv[:, :, og * 512:(og + 1) * 512]), w=[wk], dma=True)
        for j, tb in [(j_, t_) for j_ in range(4) for t_ in range(NT)]:
            xbt = xb[:, :, tb * 512:(tb + 1) * 512]
            oc0 = ocol0 + tb * 512
            tc0 = tcol0 + tb * 512
            ch = og * 4 + j
            if ch in (10, 11):
                if ch == 11:
                    continue
                for tt in range(4):
                    pb = C.psb[C.rr % 4]
                    pk = "psb%d" % (C.rr % 4)
                    C.rr += 1
                    for c in range(16):
                        P.add("pe", lambda e, c=c, tt=tt, pb=pb, wb=wb: e.matmul(
                            pb[:, 0:256], lhsT=xbt[:, c, tt * 128:(tt + 1) * 128], rhs=wb[:, c, 256:512],
                            start=(c == 0), stop=(c == 15)), r=[(xbkey, c), wk], w=[pk])
                    sg = C.stg[C.sr % 4]
                    sk = "stg%d" % (C.sr % 4)
                    C.sr += 1
                    P.add("act", lambda e, pb=pb, sg=sg: e.activation(out=sg[:, 0:256], in_=pb[:, 0:256], func=AF.Copy), r=[pk], w=[sk])
                    store(C, o_v[oc0 + tt * 128: oc0 + (tt + 1) * 128, :], sg[:, 0:256], [sk])
                continue
            pb = C.psb[C.rr % 4]
            pk = "psb%d" % (C.rr % 4)
            C.rr += 1
            for c in range(16):
                P.add("pe", lambda e, c=c, j=j, pb=pb, wb=wb: e.matmul(
                    pb[:], lhsT=wb[:, c, j * 128:(j + 1) * 128], rhs=xbt[:, c, :], start=(c == 0), stop=(c == 15)),
                    r=[(xbkey, c), wk], w=[pk])
            for t_ in pend:
                t_()
            pend.clear()
            sg = C.stg[C.sr % 4]
            sk = "stg%d" % (C.sr % 4)
            C.sr += 1
            if ch >= 12:
                eng = "act" if (ch % 2 == 0) else "dve"
                if eng == "act":
                    P.add("act", lambda e, pb=pb, sg=sg: e.activation(out=sg[:], in_=pb[:], func=AF.Copy), r=[pk], w=[sk])
                else:
                    P.add("dve", lambda e, pb=pb, sg=sg: e.tensor_copy(out=sg[:], in_=pb[:]), r=[pk], w=[sk])
                hrow = (ch - 12) * 128
                store(C, o_hy[hrow:hrow + 128, oc0:oc0 + 512], sg[:], [sk])
                continue
            gt = qg_t if ch < 8 else kg_t
            i2 = C.qr % 2
            C.qr += 1
            qg = C.qk_qg[i2]
            sq = C.qk_sq[i2]
            t1 = C.qk_t1[i2]
            t2 = C.qk_t2[i2]
            kq, ks, k1, k2 = "qkqg%d" % i2, "qksq%d" % i2, "qkt1%d" % i2, "qkt2%d" % i2
            pss = C.psb[4 + i2]
            pks = "psb%d" % (4 + i2)
            psr = C.psb[6 + i2]
            pkr = "psb%d" % (6 + i2)
            P.add("act", lambda e, pb=pb, qg=qg, gt=gt: e.activation(out=qg[:], in_=pb[:], func=AF.Identity, scale=gt[:, 0:1]), r=[pk, "qkg"], w=[kq])
            P.add("act", lambda e, pb=pb, sq=sq: e.activation(out=sq[:], in_=pb[:], func=AF.Square), r=[pk], w=[ks])
            def tail(tc0=tc0, oc0=oc0, ch=ch, sq=sq, qg=qg, t1=t1, t2=t2, sg=sg, sk=sk, pss=pss, pks=pks, psr=psr, pkr=pkr, kq=kq, ks=ks, k1=k1, k2=k2):
                P.add("pe", lambda e, sq=sq, pss=pss: e.matmul(pss[:], lhsT=C.onesH[:], rhs=sq[:], start=True, stop=True), r=[ks, "onesH"], w=[pks])
                P.add("pe", lambda e, qg=qg, psr=psr: e.matmul(psr[:], lhsT=C.rotm[:], rhs=qg[:], start=True, stop=True), r=[kq, "rotm"], w=[pkr])
                tc = slice(tc0, tc0 + 512)
                P.add("dve", lambda e, qg=qg, t1=t1, tc=tc: e.tensor_tensor(out=t1[:], in0=qg[:].bitcast(F32), in1=cos_t[:, tc], op=ALU.mult), r=[kq, "rope"], w=[k1])
                P.add("dve", lambda e, psr=psr, t2=t2, tc=tc: e.tensor_tensor(out=t2[:], in0=psr[:], in1=sin_t[:, tc], op=ALU.mult), r=[pkr, "rope2"], w=[k2])
                P.add("pool", lambda e, t1=t1, t2=t2: e.tensor_tensor(out=t1[:], in0=t1[:], in1=t2[:], op=ALU.add), r=[k1, k2], w=[k1])
                P.add("act", lambda e, pss=pss, t2=t2: e.activation(out=t2[:], in_=pss[:], func=AF.Sqrt, bias=C.eps_rms[:, 0:1]), r=[pks, "eps", k1], w=[k2])
                P.add("dve", lambda e, t2=t2: e.reciprocal(out=t2[:], in_=t2[:]), r=[k2], w=[k2])
                P.add("dve", lambda e, t1=t1, t2=t2, sg=sg: e.tensor_tensor(out=sg[:], in0=t1[:], in1=t2[:], op=ALU.mult), r=[k1, k2], w=[sk])
                if ch < 8:
                    store(C, o_q[ch * 128:(ch + 1) * 128, oc0:oc0 + 512], sg[:], [sk])
                else:
                    store(C, o_k[(ch - 8) * 128:(ch - 7) * 128, oc0:oc0 + 512], sg[:], [sk])
            pend.append(tail)
    for t_ in pend:
        t_()
    pend.clear()


def setup_inproj(C, small=False):
    P = C.P
    C.wbuf = [P.sb("wbuf%d" % i, [128, 16, 512], BF16) for i in range(2 if small else 3)]
    C.stg = [P.sb("stg%d" % i, [128, 512]) for i in range(4)]
    C.qk_qg = [P.sb("qkqg%d" % i, [128, 512], F32R) for i in range(2)]
    C.qk_sq = [P.sb("qksq%d" % i, [128, 512], F32R) for i in range(2)]
    if small:
        C.qk_t1 = [P.sb("ct%d" % i, [128, 512]) for i in range(2)]
        C.qk_t2 = [P.sb("ct%d" % (2 + i), [128, 512]) for i in range(2)]
    else:
        C.qk_t1 = [P.sb("qkt1%d" % i, [128, 512]) for i in range(2)]
        C.qk_t2 = [P.sb("qkt2%d" % i, [128, 512]) for i in range(2)]
    C.rr = 0
    C.sr = 0
    C.qr = 0


def setup_consts_A(C):
    P = C.P
    cst = din(C, "cstA", [128, 3 * 128 + 2])
    raw = load_const(C, "cstA_sb", cst, [128, 3 * 128 + 2])
    C.onesD = P.sb("onesD", [128, 128])
    C.onesH = P.sb("onesH", [128, 128], F32R)
    C.rotm = P.sb("rotm", [128, 128], F32R)
    C.eps_ln = P.sb("eps_ln", [128, 1])
    C.eps_rms = P.sb("eps_rms", [128, 1])
    P.add("dve", lambda e: e.tensor_copy(out=C.onesD[:], in_=raw[:, 0:128]), r=["cstA_sb"], w=["onesD"])
    C.onesDb = P.sb("onesDb", [128, 128], BF16)
    P.add("dve", lambda e: e.tensor_copy(out=C.onesDb[:], in_=raw[:, 0:128]), r=["cstA_sb", "onesD"], w=["onesD"])
    P.add("dve", lambda e: e.tensor_copy(out=C.onesH[:], in_=raw[:, 128:256]), r=["cstA_sb"], w=["onesH"])
    P.add("dve", lambda e: e.tensor_copy(out=C.rotm[:], in_=raw[:, 256:384]), r=["cstA_sb"], w=["rotm"])
    P.add("dve", lambda e: e.tensor_copy(out=C.eps_ln[:], in_=raw[:, 384:385]), r=["cstA_sb"], w=["eps"])
    P.add("dve", lambda e: e.tensor_copy(out=C.eps_rms[:], in_=raw[:, 385:386]), r=["cstA_sb"], w=["eps"])


def host_cstA():
    c = np.zeros((128, 386), np.float32)
    c[:, 0:128] = 1.0 / D
    c[:, 128:256] = 1.0 / 128
    R = np.zeros((128, 128), np.float32)
    for base in (0, 64):
        for m in range(32):
            R[base + m + 32, base + m] = -1.0
            R[base + m, base + m + 32] = 1.0
    c[:, 256:384] = R
    c[:, 384] = LN_EPS
    c[:, 385] = RMS_EPS
    return c


def host_rope():
    t = np.arange(L)
    row = (t // 64).astype(np.float32)
    col = (t % 64).astype(np.float32)
    inv = (10000.0 ** (-np.arange(0, 64, 2, dtype=np.float32) / 64)).astype(np.float32)
    ang = np.zeros((128, L), np.float32)
    for d in range(128):
        if d < 64:
            ang[d] = row * inv[d % 32]
        else:
            ang[d] = col * inv[(d - 64) % 32]
    return np.cos(ang).astype(np.float32), np.sin(ang).astype(np.float32)


def build_A0():
    C = new_ctx("A0")
    P = C.P
    xT = din(C, "xT", [D, TPC])
    w_in = din(C, "w_in", [D, DIN])
    lng = din(C, "lng", [128, 16])
    lnb = din(C, "lnb", [128, 16])
    qkg = din(C, "qkg", [128, 2])
    cosd = din(C, "cos", [128, TPC])
    sind = din(C, "sin", [128, TPC])
    o_q = dout(C, "qT", [1024, TPC])
    o_k = dout(C, "kT", [256, TPC])
    o_v = dout(C, "v", [TPC, 256])
    o_hy = dout(C, "hyT", [3072, TPC])
    o_x = dout(C, "x0T", [D, TPC])
    setup_common(C)
    setup_consts_A(C)
    setup_inproj(C)
    gt = load_const(C, "lng_sb", lng, [128, 16], key="lnp")
    bt = load_const(C, "lnb_sb", lnb, [128, 16], key="lnp2")
    qk = load_const(C, "qkg_sb", qkg, [128, 2], key="qkg")
    cos_t = load_const(C, "cos_sb", cosd, [128, TPC], key="rope")
    sin_t = load_const(C, "sin_sb", sind, [128, TPC], key="rope2")
    xs = P.sb("xs", [128, 16, HT])
    x32 = P.sb("x32", [128, 16, HT])
    xb = P.sb("xb", [128, 16, TPC], BF16)
    xv = xT.rearrange("(c p) t -> p c t", p=128)
    xov = o_x.rearrange("(c p) t -> p c t", p=128)
    for h in range(2):
        for c4 in range(4):
            P.add("sp", lambda e, h=h, c4=c4: e.dma_start(out=xs[:, c4 * 4:(c4 + 1) * 4, :], in_=xv[:, c4 * 4:(c4 + 1) * 4, h * HT:(h + 1) * HT]),
                  w=[("xs", c) for c in range(c4 * 4, c4 * 4 + 4)], dma=True)
        emit_ln(C, xs, "xs", HT, gt, bt, x32, xb[:, :, h * HT:(h + 1) * HT], "x", "a")
        for c4 in range(4):
            store(C, xov[:, c4 * 4:(c4 + 1) * 4, h * HT:(h + 1) * HT], x32[:, c4 * 4:(c4 + 1) * 4, :], [("x32", c) for c in range(c4 * 4, c4 * 4 + 4)])
    emit_inproj(C, xb, "xbf", w_in, qk[:, 0:1], qk[:, 1:2], cos_t, sin_t, 0, o_q, o_k, o_v, o_hy, 0, NT=2)
    finish(C)
    return C


WIN = HT + 2
PC_MIXG, PC_BOUT, PC_L1G, PC_L1B, PC_L2G, PC_L2B, PC_BDN = 0, 16, 32, 48, 64, 80, 96
PC_BG, PC_BV, PC_CW, PC_CB, PC_MASK, PC_N = 112, 155, 198, 327, 370, 374


def host_prmC(inp, l, core):
    p = np.zeros((128, PC_N), np.float32)
    f16 = lambda v: np.ascontiguousarray(v.reshape(16, 128).T)
    p[:, PC_MIXG:PC_MIXG + 16] = f16(np.concatenate([inp["att_out_g"][l], inp["hy_out_g"][l]]))
    p[:, PC_BOUT:PC_BOUT + 16] = f16(inp["b_out"][l])
    p[:, PC_L1G:PC_L1G + 16] = f16(inp["ln1_g"][l])
    p[:, PC_L1B:PC_L1B + 16] = f16(inp["ln1_b"][l])
    p[:, PC_L2G:PC_L2G + 16] = f16(inp["ln2_g"][l])
    p[:, PC_L2B:PC_L2B + 16] = f16(inp["ln2_b"][l])
    p[:, PC_BDN:PC_BDN + 16] = f16(inp["b_down"][l])
    p[:, PC_BG:PC_BG + 43] = inp["b_up"][l][:DFF].reshape(43, 128).T
    p[:, PC_BV:PC_BV + 43] = inp["b_up"][l][DFF:].reshape(43, 128).T
    for k in range(3):
        p[:, PC_CW + k * 43:PC_CW + (k + 1) * 43] = inp["ffn_conv_w"][l][k].reshape(43, 128).T
    p[:, PC_CB:PC_CB + 43] = inp["ffn_conv_b"][l].reshape(43, 128).T
    p[:, PC_MASK + 0] = 1.0 if core > 0 else 0.0
    p[:, PC_MASK + 1] = 1.0
    p[:, PC_MASK + 2] = 1.0
    p[:, PC_MASK + 3] = 1.0 if core < NC - 1 else 0.0
    return p


def emit_colstat(C, src, key, chunks, cols_list, ones, eps_ap, out, outkey, tag):
    P = C.P
    sq = [P.sb("ln_sqb%d" % i, [128, 512], BF16) for i in range(2)]
    for (cs, n, ocs) in cols_list:
        pq = C.psb[7]
        for i, c in enumerate(chunks):
            s = sq[i % 2]
            sk = "lnsq%d" % (i % 2)
            P.add("act", lambda e, c=c, s=s, cs=cs, n=n: e.activation(out=s[:, 0:n], in_=src[:, c, cs], func=AF.Square), r=[(key, c)], w=[sk])
            P.add("pe", lambda e, s=s, n=n, i=i: e.matmul(pq[:, 0:n], lhsT=ones[:], rhs=s[:, 0:n], start=(i == 0), stop=(i == len(chunks) - 1)),
                  r=[sk, "onesG"], w=["psb7"])
        P.add("act", lambda e, n=n, ocs=ocs: e.activation(out=out[:, ocs], in_=pq[:, 0:n], func=AF.Sqrt, bias=eps_ap), r=["psb7", "eps"], w=[outkey])
        P.add("dve", lambda e, ocs=ocs: e.reciprocal(out=out[:, ocs], in_=out[:, ocs]), r=[outkey], w=[outkey])


def build_C(with_inproj):
    C = new_ctx("C")
    P = C.P
    aT = din(C, "aT", [1024, TPC + 2])
    yT = din(C, "yT", [1024, TPC + 2])
    xT = din(C, "xT", [D, TPC + 2])
    w_out = din(C, "w_out", [D, D], BF16)
    w_up = din(C, "w_up", [D, 2 * DFF], BF16)
    w_down = din(C, "w_down", [DFF, D], BF16)
    prm = din(C, "prmC", [128, PC_N])
    o_x = dout(C, "x2T", [D, TPC])
    setup_common(C)
    setup_consts_A(C)
    cst2 = din(C, "cstC", [128, 128])
    onesG32 = load_const(C, "onesG", cst2, [128, 128], key="onesG32")
    C.onesG = P.sb("onesGb", [128, 128], BF16)
    P.add("dve", lambda e: e.tensor_copy(out=C.onesG[:], in_=onesG32[:]), r=["onesG32"], w=["onesG"])
    if with_inproj:
        w_in = din(C, "w_in", [D, DIN], BF16)
        C.wq = "sp"
        qkg = din(C, "qkg", [128, 2])
        cosd = din(C, "cos", [128, TPC])
        sind = din(C, "sind" if False else "sin", [128, TPC])
        o_q = dout(C, "qT", [1024, TPC])
        o_k = dout(C, "kT", [256, TPC])
        o_v = dout(C, "v", [TPC, 256])
        o_hy = dout(C, "hyT", [3072, TPC])
        qk = load_const(C, "qkg_sb", qkg, [128, 2], key="qkg")
        cos_t = P.sb("cos_sb", [128, HT])
        sin_t = P.sb("sin_sb", [128, HT])
        setup_inproj(C, small=True)
        C.wbuf.append(P.sb("wbuf2", [128, 16, 512], BF16))
    else:
        C.wbuf = [P.sb("wbuf%d" % i, [128, 16, 512], BF16) for i in range(3)]
        C.stg = [P.sb("stg%d" % i, [128, 512]) for i in range(4)]
        C.rr = 0
        C.sr = 0
    pt = load_const(C, "prm_sb", prm, [128, PC_N], key="prm")
    arena = P.sb("arena", [128, 12416])
    mix = arena[:, 0:16 * WIN].rearrange("p (c t) -> p c t", c=16)
    mixb = arena[:, 8256:8256 + 8 * WIN].bitcast(BF16).rearrange("p (c t) -> p c t", c=16)
    hT = arena[:, 0:NFF * 256].bitcast(BF16).rearrange("p (f t) -> p f t", f=NFF)
    xr = P.sb("xr", [128, 16, WIN])
    x1b = P.sb("x1b", [128, 16, WIN], BF16)
    rsa = P.sb("rsa", [128, WIN])
    rsh = P.sb("rsh", [128, WIN])
    tt = [P.sb("ct%d" % i, [128, 512]) for i in range(4)]
    gt_ = [P.sb("gt%d" % i, [128, WIN]) for i in range(2)]
    aTv = aT.rearrange("(c p) t -> p c t", p=128)
    yTv = yT.rearrange("(c p) t -> p c t", p=128)
    xTv = xT.rearrange("(c p) t -> p c t", p=128)
    xov = o_x.rearrange("(c p) t -> p c t", p=128)
    wov = w_out.rearrange("(c p) n -> p c n", p=128)
    wuv = w_up.rearrange("(c p) n -> p c n", p=128)
    wdv = w_down.rearrange("(f p) n -> p f n", p=128)
    MAIN = slice(1, 1 + HT)
    HALO = slice(0, WIN, WIN - 1)
    wq = [0]
    pre = {}

    def nextw():
        i = wq[0] % 3
        wq[0] += 1
        return C.wbuf[i], "wbuf%d" % i

    for h in range(2):
        w0 = h * HT
        for c4 in range(2):
            P.add("sp", lambda e, c4=c4, w0=w0: e.dma_start(out=mix[:, c4 * 4:(c4 + 1) * 4, :], in_=aTv[:, c4 * 4:(c4 + 1) * 4, w0:w0 + WIN]),
                  w=[("mix", c) for c in range(c4 * 4, c4 * 4 + 4)], dma=True)
            P.add("sp", lambda e, c4=c4, w0=w0: e.dma_start(out=mix[:, 8 + c4 * 4:8 + (c4 + 1) * 4, :], in_=yTv[:, c4 * 4:(c4 + 1) * 4, w0:w0 + WIN]),
                  w=[("mix", c) for c in range(8 + c4 * 4, 8 + c4 * 4 + 4)], dma=True)
        for c4 in range(4):
            P.add("sp", lambda e, c4=c4, w0=w0: e.dma_start(out=xr[:, c4 * 4:(c4 + 1) * 4, :], in_=xTv[:, c4 * 4:(c4 + 1) * 4, w0:w0 + WIN]),
                  w=[("xr", c) for c in range(c4 * 4, c4 * 4 + 4)], dma=True)
        cols = [(MAIN, HT, MAIN), (HALO, 2, HALO)]
        emit_colstat(C, mix, "mix", list(range(8)), cols, C.onesG, C.eps_rms[:, 0:1], rsa, "rsa", "a")
        emit_colstat(C, mix, "mix", list(range(8, 16)), cols, C.onesG, C.eps_rms[:, 0:1], rsh, "rsh", "h")
        for c in range(16):
            rs_, rsk = (rsa, "rsa") if c < 8 else (rsh, "rsh")
            P.add("dve", lambda e: e.scalar_tensor_tensor(out=mixb[:, c, :], in0=mix[:, c, :], scalar=pt[:, PC_MIXG + c:PC_MIXG + c + 1], in1=rs_[:],
                                                          op0=ALU.mult, op1=ALU.mult), r=[("mix", c), "prm", rsk], w=[("mixb", c)])
            P.add("act", lambda e: e.activation(out=xr[:, c, :], in_=xr[:, c, :], func=AF.Copy, scale=float(ALPHA)), r=[("xr", c)], w=[("xr", c)])
        for og in range(4):
            if h == 0 and og < 2 and ("oproj", og) in pre:
                wb, wk = pre.pop(("oproj", og))
            elif ("oproj", og) in pre:
                wb, wk = pre.pop(("oproj", og))
            else:
                wb, wk = nextw()
                P.add("sp", lambda e: e.dma_start(out=wb[:], in_=wov[:, :, og * 512:(og + 1) * 512]), w=[wk], dma=True)
            for j in range(4):
                o = og * 4 + j
                b2 = 2 * (o % 3)
                pm, pkm = C.psb[b2], "psb%d" % b2
                pz, pkz = C.psb[b2 + 1], "psb%d" % (b2 + 1)
                for c in range(16):
                    P.add("pe", lambda e: e.matmul(pm[:], lhsT=wb[:, c, j * 128:(j + 1) * 128], rhs=mixb[:, c, MAIN], start=(c == 0), stop=(c == 15)),
                          r=[("mixb", c), wk], w=[pkm])
                for c in range(16):
                    P.add("pe", lambda e: e.matmul(pz[:, 0:2], lhsT=wb[:, c, j * 128:(j + 1) * 128], rhs=mixb[:, c, HALO], start=(c == 0), stop=(c == 15)),
                          r=[("mixb", c), wk], w=[pkz])
                bo = pt[:, PC_BOUT + o:PC_BOUT + o + 1]
                P.add("dve", lambda e: e.scalar_tensor_tensor(out=xr[:, o, MAIN], in0=pm[:], scalar=bo, in1=xr[:, o, MAIN], op0=ALU.add, op1=ALU.add),
                      r=[pkm, ("xr", o), "prm"], w=[("xr", o)])
                P.add("dve", lambda e: e.scalar_tensor_tensor(out=xr[:, o, HALO], in0=pz[:, 0:2], scalar=bo, in1=xr[:, o, HALO], op0=ALU.add, op1=ALU.add),
                      r=[pkz, ("xr", o), "prm"], w=[("xr", o)])
        for g2 in range(2):
            wb, wk = nextw()
            P.add("sp", lambda e: e.dma_start(out=wb[:, :, 0:256], in_=wuv[:, :, g2 * 256:g2 * 256 + 256]), w=[wk + "g", wk], dma=True)
            P.add("sp", lambda e: e.dma_start(out=wb[:, :, 256:512], in_=wuv[:, :, DFF + g2 * 256:DFF + g2 * 256 + 256]), w=[wk + "v", wk], dma=True)
            pre[("up", g2)] = (wb, wk)
        C.lnp_g = pt[:, PC_L1G:PC_L1G + 16]
        emit_ln(C, xr, "xr", WIN, pt[:, PC_L1G:PC_L1G + 16], pt[:, PC_L1B:PC_L1B + 16], xr, x1b, "x1", "c", pkeys=("prm", "prm"))
        P.barrier()
        for g2 in range(22):
            nf = 2 if g2 < 21 else 1
            if ("up", g2) in pre:
                wb, wk = pre.pop(("up", g2))
            else:
                wb, wk = nextw()
                P.add("sp", lambda e, g2=g2, wb=wb, nf=nf: e.dma_start(out=wb[:, :, 0:nf * 128], in_=wuv[:, :, g2 * 256:g2 * 256 + nf * 128]), w=[wk + "g"], dma=True)
                P.add("sp", lambda e, g2=g2, wb=wb, nf=nf: e.dma_start(out=wb[:, :, 256:256 + nf * 128], in_=wuv[:, :, DFF + g2 * 256:DFF + g2 * 256 + nf * 128]), w=[wk + "v"], dma=True)
            for j in range(nf):
                f = g2 * 2 + j
                b3 = (f % 2) * 3
                pg, pkg = C.psb[b3], "psb%d" % b3
                pv, pkv = C.psb[b3 + 1], "psb%d" % (b3 + 1)
                pz, pkz = C.psb[b3 + 2], "psb%d" % (b3 + 2)
                for c in range(16):
                    P.add("pe", lambda e, c=c, j=j, wb=wb, pg=pg: e.matmul(pg[:], lhsT=wb[:, c, j * 128:(j + 1) * 128], rhs=x1b[:, c, MAIN], start=(c == 0), stop=(c == 15)),
                          r=[("x1bf", c), wk + "g"], w=[pkg])
                for c in range(16):
                    P.add("pe", lambda e, c=c, j=j, wb=wb, pz=pz: e.matmul(pz[:, 0:2], lhsT=wb[:, c, j * 128:(j + 1) * 128], rhs=x1b[:, c, HALO], start=(c == 0), stop=(c == 15)),
                          r=[("x1bf", c), wk + "g"], w=[pkz])
                for c in range(16):
                    P.add("pe", lambda e, c=c, j=j, wb=wb, pv=pv: e.matmul(pv[:], lhsT=wb[:, c, 256 + j * 128:256 + (j + 1) * 128], rhs=x1b[:, c, MAIN], start=(c == 0), stop=(c == 15)),
                          r=[("x1bf", c), wk + "v"], w=[pkv])
                gt = gt_[f % 2]
                gk = "gt%d" % (f % 2)
                ct = tt[f % 2]
                ck = "ct%d" % (f % 2)
                gl = tt[2 + f % 2]
                glk = "ct%d" % (2 + f % 2)
                bgp = pt[:, PC_BG + f:PC_BG + f + 1]
                bvp = pt[:, PC_BV + f:PC_BV + f + 1]
                P.add("act", lambda e, pg=pg, gt=gt, bgp=bgp: e.activation(out=gt[:, MAIN], in_=pg[:], func=AF.Identity, bias=bgp), r=[pkg, "prm"], w=[gk])
                P.add("dve", lambda e, pz=pz, gt=gt, bgp=bgp, h=h: e.scalar_tensor_tensor(out=gt[:, HALO], in0=pz[:, 0:2], scalar=bgp, in1=pt[:, PC_MASK + 2 * h:PC_MASK + 2 * h + 2],
                                                                                  op0=ALU.add, op1=ALU.mult), r=[pkz, "prm", gk], w=[gk])
                P.add("dve", lambda e, gt=gt, ct=ct, f=f: e.tensor_scalar(out=ct[:], in0=gt[:, 0:HT], scalar1=pt[:, PC_CW + f:PC_CW + f + 1], scalar2=pt[:, PC_CB + f:PC_CB + f + 1],
                                                                          op0=ALU.mult, op1=ALU.add), r=[gk, "prm"], w=[ck])
                P.add("dve", lambda e, gt=gt, ct=ct, f=f: e.scalar_tensor_tensor(out=ct[:], in0=gt[:, 1:1 + HT], scalar=pt[:, PC_CW + 43 + f:PC_CW + 43 + f + 1], in1=ct[:],
                                                                                  op0=ALU.mult, op1=ALU.add), r=[gk, ck, "prm"], w=[ck])
                P.add("dve", lambda e, gt=gt, ct=ct, f=f: e.scalar_tensor_tensor(out=ct[:], in0=gt[:, 2:2 + HT], scalar=pt[:, PC_CW + 86 + f:PC_CW + 86 + f + 1], in1=ct[:],
                                                                                 op0=ALU.mult, op1=ALU.add), r=[gk, ck, "prm"], w=[ck])
                P.add("act", lambda e, ct=ct, gl=gl: e.activation(out=gl[:], in_=ct[:], func=AF.Gelu), r=[ck], w=[glk])
                P.add("dve", lambda e, pv=pv, gl=gl, f=f, bvp=bvp: e.scalar_tensor_tensor(out=hT[:, f, :], in0=pv[:], scalar=bvp, in1=gl[:], op0=ALU.add, op1=ALU.mult),
                      r=[pkv, glk, "prm"], w=[("hT", f)])
        for o in range(16):
            wb, wk = nextw()
            wdt = wb[:].rearrange("p c n -> p (c n)")[:, 0:NFF * 128].rearrange("p (f n) -> p f n", f=NFF)
            P.add("sp", lambda e, o=o, wdt=wdt: e.dma_start(out=wdt, in_=wdv[:, :, o * 128:(o + 1) * 128]), w=[wk + "g", wk + "v", wk], dma=True)
            pd, pkd = C.psb[o % 4], "psb%d" % (o % 4)
            for f in range(NFF):
                P.add("pe", lambda e, f=f, wdt=wdt, pd=pd: e.matmul(pd[:], lhsT=wdt[:, f, :], rhs=hT[:, f, :], start=(f == 0), stop=(f == NFF - 1)),
                      r=[("hT", f), wk], w=[pkd])
            t1 = tt[o % 4]
            k1 = "ct%d" % (o % 4)
            P.add("act", lambda e, pd=pd, t1=t1, o=o: e.activation(out=t1[:], in_=pd[:], func=AF.Identity, bias=pt[:, PC_BDN + o:PC_BDN + o + 1]), r=[pkd, "prm"], w=[k1])
            P.add("dve", lambda e, t1=t1, o=o: e.scalar_tensor_tensor(out=xr[:, o, MAIN], in0=xr[:, o, MAIN], scalar=float(ALPHA), in1=t1[:], op0=ALU.mult, op1=ALU.add),
                  r=[k1, ("x132", o)], w=[("x132", o)])
        x2 = arena[:, 0:16 * HT].rearrange("p (c t) -> p c t", c=16)
        x2b = x1b[:, :, 0:HT] if with_inproj else None
        emit_ln(C, xr, "x132", HT, pt[:, PC_L2G:PC_L2G + 16], pt[:, PC_L2B:PC_L2B + 16], x2, x2b, "x2", "d", pkeys=("prm", "prm"), c0=1)
        for c4 in range(4):
            store(C, xov[:, c4 * 4:(c4 + 1) * 4, h * HT:(h + 1) * HT], x2[:, c4 * 4:(c4 + 1) * 4, :], [("x232", c) for c in range(c4 * 4, c4 * 4 + 4)])
        P.barrier()
        if with_inproj:
            P.add("sp", lambda e: e.dma_start(out=cos_t[:], in_=cosd[:, h * HT:(h + 1) * HT]), w=["rope"], dma=True)
            P.add("sp", lambda e: e.dma_start(out=sin_t[:], in_=sind[:, h * HT:(h + 1) * HT]), w=["rope2"], dma=True)
            emit_inproj(C, x2b, "x2bf", w_in, qk[:, 0:1], qk[:, 1:2], cos_t, sin_t, 0, o_q, o_k, o_v, o_hy, h * HT)
            P.barrier()
    finish(C)
    return C


def emit_attention(C, qT_d, kT_d, v_d, o_a, cstB_t):
    P = C.P
    NKB = L // 128
    import os
    NQB = int(os.environ.get('DBG_NQB', L // 512))
    AR, A32 = C.AR, C.A32
    AR.reset()
    A32.reset()
    ks = AR.get([128, L])
    vs = AR.get([128, NKB, 128])
    qblk = [AR.get([128, 512]) for i in range(2)]
    pT = [AR.get([128, 512]) for i in range(4)]
    onesK = P.sb("att_ones", [128, 128], F32R)
    stgq = [A32.get([128, 2048]) for i in range(3)]
    qst = [A32.get([128, 512]) for i in range(2)]
    rl = A32.get([128, 512])
    ob = [A32.get([128, 512]) for i in range(2)]
    P.add("dve", lambda e: e.tensor_copy(out=onesK[:], in_=cstB_t[:]), r=["cstB"], w=["att_ones"])
    vv = v_d.rearrange("(b p) d -> p b d", p=128)
    n = 0
    for i in range(4):
        sl = slice(i * 2048, (i + 1) * 2048)
        bs = slice(i * 16, (i + 1) * 16)
        for (kind, key) in (("k", ("ak", i)), ("v", ("av", i))):
            sg = stgq[n % 3]
            sk = "attstg%d" % (n % 3)
            n += 1
            if kind == "k":
                P.add("sp", lambda e: e.dma_start(out=sg[:], in_=kT_d[:, sl]), w=[sk], dma=True)
                P.add("dve", lambda e: e.tensor_copy(out=ks[:, sl], in_=sg[:]), r=[sk], w=[key])
            else:
                P.add("sp", lambda e: e.dma_start(out=sg[:].rearrange("p (b d) -> p b d", d=128), in_=vv[:, bs, :]), w=[sk], dma=True)
                P.add("pool", lambda e: e.tensor_copy(out=vs[:, bs, :], in_=sg[:].rearrange("p (b d) -> p b d", d=128)), r=[sk], w=[key])
    scale = 1.0 / math.sqrt(128.0)
    LOOK = 2
    for qb in range(NQB):
        qsl = slice(qb * 512, (qb + 1) * 512)
        qs, qk_ = qblk[qb % 2], "attq%d" % (qb % 2)
        P.add("sp", lambda e: e.dma_start(out=qst[qb % 2][:], in_=qT_d[:, qsl]), w=["attqst%d" % (qb % 2)], dma=True)
        P.add("pool", lambda e: e.tensor_copy(out=qs[:], in_=qst[qb % 2][:]), r=["attqst%d" % (qb % 2)], w=[qk_])
        po, pko = C.psb[4 + qb % 2], "psb%d" % (4 + qb % 2)
        pl, pkl = C.psb[6 + qb % 2], "psb%d" % (6 + qb % 2)

        def smm(kb):
            ps, pks = C.psb[kb % 4], "psb%d" % (kb % 4)
            P.add("pe", lambda e, kb=kb, ps=ps: e.matmul(ps[:], lhsT=ks[:, kb * 128:(kb + 1) * 128], rhs=qs[:], start=True, stop=True),
                  r=[("ak", kb // 16), qk_], w=[pks])
            pt_, ptk = pT[kb % 4], "attp%d" % (kb % 4)
            P.add("act", lambda e, ps=ps, pt_=pt_: e.activation(out=pt_[:], in_=ps[:], func=AF.Exp, scale=scale), r=[pks], w=[ptk])

        for kb in range(min(LOOK, NKB)):
            smm(kb)
        for kb in range(NKB):
            if kb + LOOK < NKB:
                smm(kb + LOOK)
            pt_, ptk = pT[kb % 4], "attp%d" % (kb % 4)
            P.add("pe", lambda e, kb=kb, pt_=pt_: e.matmul(po[:], lhsT=vs[:, kb, :], rhs=pt_[:], start=(kb == 0), stop=(kb == NKB - 1)),
                  r=[("av", kb // 16), ptk], w=[pko])
            P.add("pe", lambda e, kb=kb, pt_=pt_: e.matmul(pl[:], lhsT=onesK[:], rhs=pt_[:], start=(kb == 0), stop=(kb == NKB - 1)),
                  r=["att_ones", ptk], w=[pkl])
        o_ = ob[qb % 2]
        ok = "atto%d" % (qb % 2)
        P.add("dve", lambda e: e.reciprocal(out=rl[:], in_=pl[:]), r=[pkl], w=["att_rl"])
        P.add("dve", lambda e, o_=o_: e.tensor_tensor(out=o_[:], in0=po[:], in1=rl[:], op=ALU.mult), r=[pko, "att_rl"], w=[ok])
        store(C, o_a[:, qsl], o_[:], [ok])
        if os.environ.get('DBG_BAR'):
            P.barrier()


def build_B(do_att=True, do_hy=True):
    C = new_ctx("B")
    P = C.P
    setup_common(C)
    cstB = din(C, "cstB", [128, 128])
    cstB_t = load_const(C, "cstB_sb", cstB, [128, 128], key="cstB")
    C.AR = Arena(P.sb("AR", [128, 23616], F32R), 23616)
    C.A32 = Arena(P.sb("A32", [128, 8704]), 8704)
    if do_att:
        qT = din(C, "qT", [128, L])
        kT = din(C, "kT", [128, L])
        v = din(C, "v", [L, 128])
        o_a = dout(C, "aT", [128, L])
        emit_attention(C, qT, kT, v, o_a, cstB_t)
        P.barrier()
    if do_att and do_hy:
        for nm, rows, cols in (("wo", 256, D), ("wu", 256, 2 * DFF), ("wd", DFF // NC, D), ("wi", 256, DIN)):
            src = din(C, nm + "_f32", [rows, cols])
            dst = dout(C, nm + "_bf", [rows, cols], BF16)
            nsp = 4 if nm == "wu" else 1
            rs = rows // nsp
            for i in range(nsp):
                C.uid += 1
                k = "out%d" % C.uid
                C.outkeys.append(k)
                P.add("pool", lambda e: e.dma_start(out=dst[i * rs:(i + 1) * rs, :], in_=src[i * rs:(i + 1) * rs, :]), w=[k], dma=True)
    if do_hy:
        emit_hyena(C)
    finish(C)
    return C


NFFT = 2 * L
HP_N = 18


def host_fftc():
    c = np.zeros((128, 1792), np.float64)
    hi = np.arange(64)[:, None]
    k1 = np.arange(128)[None, :]
    c[0:64, 0:128] = np.cos(2 * np.pi * hi * k1 / 128)
    c[0:64, 128:256] = -np.sin(2 * np.pi * hi * k1 / 128)
    p = np.arange(128)[:, None]
    f = np.arange(128)[None, :]
    c[:, 256:384] = np.cos(2 * np.pi * p * f / NFFT)
    c[:, 384:512] = np.sin(2 * np.pi * p * f / NFFT)
    C3 = np.cos(2 * np.pi * p * f / 128)
    S3 = np.sin(2 * np.pi * p * f / 128)
    c[:, 512:640] = C3
    c[:, 640:768] = S3
    c[:, 768:896] = -S3
    c[:, 896:960] = C3[:, 0:64] / NFFT
    c[:, 960:1024] = -S3[:, 0:64] / NFFT
    c[:, 1024:1152] = -S3
    c[:, 1152:1280] = C3
    c[:, 1280:1408] = -C3
    c[:, 1408:1536] = -C3
    c[:, 1536:1664] = -S3
    c[:, 1664:1728] = -C3[:, 0:64] / NFFT
    return c.astype(np.float32)


def host_zT():
    bands = 16
    t = np.linspace(0.0, 1.0, L, dtype=np.float32)[:, None]
    w = (2.0 * np.float32(math.pi) * np.arange(L, dtype=np.float32)[:, None] / np.float32(L)).astype(np.float32)
    f = np.linspace(1e-4, bands - 1, bands, dtype=np.float32)[None, :]
    z = np.concatenate([t, np.cos(f * w), -np.sin(f * w)], axis=-1).astype(np.float32)
    return np.ascontiguousarray(z.T)


def host_hy_inputs(inp, l, j, hyT):
    sl = slice(128 * j, 128 * j + 128)
    hu = np.concatenate([hyT[128 * j + 1024 * g: 128 * j + 1024 * g + 128] for g in range(3)], 0)
    hp = np.zeros((128, HP_N), np.float32)
    for g in range(3):
        for k in range(3):
            hp[:, g * 3 + k] = inp["hy_conv_w"][l][k, 1024 * g + 128 * j: 1024 * g + 128 * j + 128]
        hp[:, 9 + g] = inp["hy_conv_b"][l][1024 * g + 128 * j: 1024 * g + 128 * j + 128]
    hp[:, 12] = inp["filt_decay"][l][0, sl]
    hp[:, 13] = inp["filt_decay"][l][1, sl]
    hp[:, 14] = inp["hy_skip"][l][sl]
    hp[:, 16] = inp["filt_b3"][l][sl]
    hp[:, 17] = inp["filt_b3"][l][1024 + 128 * j: 1024 + 128 * j + 128]
    fw3 = np.concatenate([inp["filt_w3"][l][:, sl], inp["filt_w3"][l][:, 1024 + 128 * j: 1024 + 128 * j + 128]], 1)
    fpar = np.stack([inp["filt_b1"][l], inp["filt_f1"][l], inp["filt_b2"][l], inp["filt_f2"][l]], 1)
    tposm = np.tile(np.arange(512, dtype=np.float32)[None, :], (128, 1))
    blkv = np.tile((512.0 * np.arange(16, dtype=np.float32))[None, :], (128, 1))
    return {"hu": np.ascontiguousarray(hu), "hyp": hp, "fw1": np.ascontiguousarray(inp["filt_w1"][l]), "fw2": np.ascontiguousarray(inp["filt_w2"][l]),
            "fw3": np.ascontiguousarray(fw3), "fpar": np.ascontiguousarray(fpar), "zT": host_zT(), "tposm": tposm, "blkv": blkv, "fftc": host_fftc()}


def emit_hyena(C):
    import os
    P = C.P
    nc = C.nc
    hu = din(C, "hu", [384, L])
    hyp = din(C, "hyp", [128, HP_N])
    fw1 = din(C, "fw1", [33, 64])
    fw2 = din(C, "fw2", [64, 64])
    fw3 = din(C, "fw3", [64, 256])
    fpar = din(C, "fpar", [64, 4])
    zTd = din(C, "zT", [33, L])
    tposd = din(C, "tposm", [128, 512])
    blkd = din(C, "blkv", [128, 16])
    fftd = din(C, "fftc", [128, 1792])
    o_y = dout(C, "yT", [128, L])
    scr = {n: nc.dram_tensor("scr_" + n, [128, L], F32, kind="Internal").ap() for n in ("hf", "hb", "z", "y")}
    hp = load_const(C, "hyp_sb", hyp, [128, HP_N], key="hyp")
    w1t = load_const(C, "fw1_sb", fw1, [33, 64], key="fw")
    w2t = load_const(C, "fw2_sb", fw2, [64, 64], key="fw")
    w3t = load_const(C, "fw3_sb", fw3, [64, 256], key="fw")
    fpt = load_const(C, "fpar_sb", fpar, [64, 4], key="fpar")
    tpos = load_const(C, "tpos_sb", tposd, [128, 512], key="tpos")
    blkt = load_const(C, "blk_sb", blkd, [128, 16], key="blk")
    fft32 = load_const(C, "fft_sb", fftd, [128, 1792], key="fft32")
    AR, A32 = C.AR, C.A32
    AR.reset()
    A32.reset()
    fftr = AR.get([128, 1792])
    P.add("dve", lambda e: e.tensor_copy(out=fftr[:], in_=fft32[:]), r=["fft32"], w=["fftr"])
    tcr = P.sb("tcr", [128, 4, 128])
    tsr = P.sb("tsr", [128, 4, 128])
    for c in range(4):
        P.add("pool", lambda e: e.tensor_copy(out=tcr[:, c, :], in_=fft32[:, 256:384]), r=["fft32"], w=["tcr"])
        P.add("pool", lambda e: e.tensor_copy(out=tsr[:, c, :], in_=fft32[:, 384:512]), r=["fft32"], w=["tsr"])
    big1 = P.sb("hy_big1", [128, L])
    big2 = P.sb("hy_big2", [128, L])
    CH = 2048
    sgi = [0]

    def nstg():
        i = sgi[0] % 3
        sgi[0] += 1
        return stg[i], "hystg%d" % i

    par = P.sb("hy_par", [128, 48])
    P.add("dve", lambda e: e.tensor_scalar(out=par[0:64, 0:1], in0=fpt[:, 1:2], scalar1=1.0 / 3.0, scalar2=None, op0=ALU.mult), r=["fpar"], w=["par"])
    P.add("dve", lambda e: e.tensor_tensor(out=par[0:64, 1:2], in0=fpt[:, 0:1], in1=par[0:64, 0:1], op=ALU.mult), r=["fpar", "par"], w=["par"])
    P.add("dve", lambda e: e.tensor_scalar(out=par[0:64, 2:3], in0=fpt[:, 3:4], scalar1=1.0 / 3.0, scalar2=None, op0=ALU.mult), r=["fpar", "par"], w=["par"])
    P.add("dve", lambda e: e.tensor_tensor(out=par[0:64, 3:4], in0=fpt[:, 2:3], in1=par[0:64, 2:3], op=ALU.mult), r=["fpar", "par"], w=["par"])
    nsc = P.sb("hy_nsc", [128, 2])
    P.add("act", lambda e: e.activation(out=nsc[:], in_=hp[:, 12:14], func=AF.Abs), r=["hyp"], w=["nsc"])
    P.add("dve", lambda e: e.tensor_scalar(out=nsc[:], in0=nsc[:], scalar1=-1.0 / (L - 1), scalar2=None, op0=ALU.mult), r=["nsc"], w=["nsc"])
    P.add("dve", lambda e: e.tensor_copy(out=par[:, 4:6], in_=nsc[:]), r=["nsc", "par"], w=["par"])
    for d in range(2):
        P.add("dve", lambda e: e.tensor_scalar(out=par[:, 8 + 16 * d:24 + 16 * d], in0=blkt[:], scalar1=nsc[:, d:d + 1], scalar2=None, op0=ALU.mult),
              r=["blk", "par", "nsc"], w=["par"])
    if os.environ.get("DBG_STOP") == "1":
        return
    nacc = P.sb("hy_nacc", [128, 32])
    h1 = A32.get([64, CH])
    h2 = A32.get([64, CH])
    tmp = [A32.get([128, 512]) for i in range(4)]
    zt = A32.get([33, CH])
    for b4 in range(L // CH):
        P.add("sp", lambda e: e.dma_start(out=zt[:], in_=zTd[:, b4 * CH:(b4 + 1) * CH]), w=["zt"], dma=True)
        for (src, srck, wt, K_, dst, dstk, pc) in ((zt, "zt", w1t, 33, h1, "h1", 0), (h1, "h1", w2t, 64, h2, "h2", 2)):
            for s in range(CH // 512):
                ps, pk = C.psb[s % 4], "psb%d" % (s % 4)
                ss = slice(s * 512, (s + 1) * 512)
                P.add("pe", lambda e: e.matmul(ps[0:64, :], lhsT=wt[0:K_, :], rhs=src[0:K_, ss], start=True, stop=True), r=[srck if srck == "zt" else (srck, s), "fw"], w=[pk])
                a, ak = tmp[s % 2], "hytmp%d" % (s % 2)
                b, bk = tmp[2 + s % 2], "hytmp%d" % (2 + s % 2)
                P.add("act", lambda e: e.activation(out=a[0:64, :], in_=ps[0:64, :], func=AF.Sin, scale=par[0:64, pc:pc + 1], bias=par[0:64, pc + 1:pc + 2]),
                      r=[pk, "par"], w=[ak])
                P.add("dve", lambda e: e.tensor_tensor(out=b[0:64, :], in0=a[0:64, :], in1=a[0:64, :], op=ALU.mult), r=[ak], w=[bk])
                P.add("dve", lambda e: e.tensor_scalar(out=b[0:64, :], in0=b[0:64, :], scalar1=-4.0, scalar2=3.0, op0=ALU.mult, op1=ALU.add), r=[bk], w=[bk])
                P.add("dve", lambda e: e.tensor_tensor(out=dst[:, ss], in0=a[0:64, :], in1=b[0:64, :], op=ALU.mult), r=[ak, bk], w=[(dstk, s)])
        for d, big in ((0, big1), (1, big2)):
            for s in range(CH // 512):
                blk = b4 * 4 + s
                ps, pk = C.psb[4 + s % 4], "psb%d" % (4 + s % 4)
                ss = slice(s * 512, (s + 1) * 512)
                gs = slice(blk * 512, (blk + 1) * 512)
                P.add("pe", lambda e: e.matmul(ps[:], lhsT=w3t[:, d * 128:(d + 1) * 128], rhs=h2[:, ss], start=True, stop=True), r=[("h2", s), "fw"], w=[pk])
                a, ak = tmp[s % 2], "hytmp%d" % (s % 2)
                P.add("act", lambda e: e.activation(out=a[:], in_=tpos[:], func=AF.Exp, scale=par[:, 4 + d:5 + d], bias=par[:, 8 + 16 * d + blk:9 + 16 * d + blk]),
                      r=["tpos", "par"], w=[ak])
                P.add("dve", lambda e: e.scalar_tensor_tensor(out=big[:, gs], in0=ps[:], scalar=hp[:, 16 + d:17 + d], in1=a[:], op0=ALU.add, op1=ALU.mult),
                      r=[pk, ak, "hyp"], w=[("big%d" % (d + 1), blk // 4)])
                b, bk = tmp[2 + s % 2], "hytmp%d" % (2 + s % 2)
                P.add("act", lambda e: e.activation(out=b[:], in_=big[:, gs], func=AF.Abs, accum_out=nacc[:, d * 16 + blk:d * 16 + blk + 1]),
                      r=[("big%d" % (d + 1), blk // 4)], w=[bk, "nacc"])
    if os.environ.get("DBG_STOP") == "2":
        return
    P.add("dve", lambda e: e.tensor_reduce(out=par[:, 40:41], in_=nacc[:], axis=AX.X, op=ALU.add), r=["nacc", "par"], w=["par"])
    P.add("dve", lambda e: e.reciprocal(out=par[:, 41:42], in_=par[:, 40:41]), r=["par"], w=["par"])
    P.add("dve", lambda e: e.memset(big2[:, 0:1], 0.0), r=[("big2", 0)], w=[("big2", 0)])
    for q4 in range(4):
        qs_ = slice(q4 * CH, (q4 + 1) * CH)
        P.add("sp", lambda e: e.dma_start(out=scr["hf"][:, qs_], in_=big1[:, qs_]), r=[("big1", q4)], w=[("s_hf", q4)], dma=True)
        P.add("sp", lambda e: e.dma_start(out=scr["hb"][:, qs_], in_=big2[:, qs_]), r=[("big2", q4)], w=[("s_hb", q4)], dma=True)
    P.barrier()

    A32.reset()
    stg = [A32.get([128, CH + 2]) for i in range(3)]
    def conv_group(g, dst, dstkey):
        for q4 in range(4):
            sg, sk = nstg()
            lo = q4 * CH - 1
            hi_ = q4 * CH + CH + 1
            slo, shi = max(lo, 0), min(hi_, L)
            if q4 == 0:
                P.add("dve", lambda e: e.memset(sg[:, 0:1], 0.0), w=[sk])
            if q4 == 3:
                P.add("dve", lambda e: e.memset(sg[:, CH + 1:CH + 2], 0.0), w=[sk])
            P.add("sp", lambda e: e.dma_start(out=sg[:, slo - lo:slo - lo + (shi - slo)], in_=hu[g * 128:(g + 1) * 128, slo:shi]), r=[sk], w=[sk], dma=True)
            qs_ = slice(q4 * CH, (q4 + 1) * CH)
            dk = (dstkey, q4)
            P.add("act", lambda e: e.activation(out=dst[:, qs_], in_=sg[:, 1:CH + 1], func=AF.Identity, scale=hp[:, g * 3 + 1:g * 3 + 2], bias=hp[:, 9 + g:10 + g]),
                  r=[sk, "hyp"], w=[dk])
            P.add("dve", lambda e: e.scalar_tensor_tensor(out=dst[:, qs_], in0=sg[:, 0:CH], scalar=hp[:, g * 3:g * 3 + 1], in1=dst[:, qs_], op0=ALU.mult, op1=ALU.add),
                  r=[sk, dk, "hyp"], w=[dk])
            P.add("dve", lambda e: e.scalar_tensor_tensor(out=dst[:, qs_], in0=sg[:, 2:CH + 2], scalar=hp[:, g * 3 + 2:g * 3 + 3], in1=dst[:, qs_], op0=ALU.mult, op1=ALU.add),
                  r=[sk, dk, "hyp"], w=[dk])

    conv_group(1, big1, "big1")
    conv_group(2, big2, "big2")
    for q4 in range(4):
        qs_ = slice(q4 * CH, (q4 + 1) * CH)
        P.add("pool", lambda e: e.tensor_tensor(out=big1[:, qs_], in0=big1[:, qs_], in1=big2[:, qs_], op=ALU.mult), r=[("big1", q4), ("big2", q4)], w=[("big1", q4)])
        P.add("sp", lambda e: e.dma_start(out=scr["z"][:, qs_], in_=big1[:, qs_]), r=[("big1", q4)], w=[("s_z", q4)], dma=True)
    conv_group(0, big2, "big2")
    P.barrier()

    F1r = fftr[:, 0:256]
    C3r = fftr[:, 512:640]
    S3r = fftr[:, 640:768]
    nS3r = fftr[:, 768:896]
    nC3r = fftr[:, 1280:1408]
    C7r = fftr[:, 896:1024]
    nS7r = fftr[:, 960:1088]
    nC7r = fftr[:, 1664:1792]
    F5a = fftr[:, 512:768]
    F5b = fftr[:, 1024:1280]
    nF5a = fftr[:, 1408:1664]
    A32.reset()
    sin_ = [A32.get([64, 3, 4, 128]) for i in range(2)]
    absb = A32.get([128, 4, 2, 128])
    kre = A32.get([128, 4, 128])
    kim = A32.get([128, 4, 128])
    yst = [A32.get([64, 4, 128]) for i in range(2)]
    sr_ = [AR.get([128, 3, 4, 128]) for i in range(2)]
    for i in range(2):
        for j in range(3):
            P.add("dve", lambda e: e.tensor_scalar(out=sr_[i][64:128, j, :, :], in0=tcr[64:128, :, :], scalar1=0.0, scalar2=None, op0=ALU.mult),
                  r=["tcr"], w=["ffinr%d" % i])
    TS = [{(q, n): AR.get([128, 4, 128]) for q in "zfb" for n in range(4)} for i in range(2)]
    MM2 = [[AR.get([128, 4, 128]) for n in range(4)] for i in range(2)]
    UU = [AR.get([128, 4, 128]) for n in range(4)]
    psall = C.psall

    def pv(b0):
        return psall[:, b0 * 512:(b0 + 2) * 512].rearrange("p (c r k) -> p c r k", c=4, r=2), ["psb%d" % b0, "psb%d" % (b0 + 1)]

    def prods(eng, dst, dkeys, are, aim, akeys):
        for n, (a, t, tk) in enumerate(((are, tcr, "tcr"), (aim, tsr, "tsr"), (aim, tcr, "tcr"), (are, tsr, "tsr"))):
            P.add(eng, lambda e: e.tensor_tensor(out=dst[n][:], in0=a, in1=t[:], op=ALU.mult), r=akeys + [tk], w=[dkeys[n]])

    def s1(g, i, b0):
        sr, srk = sr_[g % 2], "ffinr%d" % (g % 2)
        for c in range(4):
            b = b0 + c // 2
            P.add("pe", lambda e: e.matmul(psall[:, b * 512 + (c % 2) * 256: b * 512 + (c % 2) * 256 + 256], lhsT=sr[:, i, c, :], rhs=F1r, start=True, stop=True),
                  r=[srk, "fftr"], w=["psb%d" % b])

    def acc(bank, terms, npart=128):
        n = len(terms)
        for i, (l, rt, rk_) in enumerate(terms):
            P.add("pe", lambda e: e.matmul(C.psb[bank][0:npart, :], lhsT=l, rhs=rt[:].rearrange("p c k -> p (c k)"), start=(i == 0), stop=(i == n - 1)),
                  r=[rk_, "fftr"], w=["psb%d" % bank])

    NG = int(os.environ.get("DBG_NG", 32))

    def ffload(gg):
        c0 = 4 * gg
        si, sik = sin_[gg % 2], "ffin%d" % (gg % 2)
        sr, srk = sr_[gg % 2], "ffinr%d" % (gg % 2)
        for i, n in enumerate(("z", "hf", "hb")):
            P.add("sp", lambda e: e.dma_start(out=si[:, i, :, :], in_=scr[n][c0:c0 + 4, :].rearrange("c (h l) -> h c l", l=128)),
                  r=[("s_" + n, q) for q in range(4)], w=[(sik, i)], dma=True)
        P.add("act", lambda e: e.activation(out=sr[0:64], in_=si[:], func=AF.Copy), r=[(sik, i) for i in range(3)], w=[srk])

    for g in range(NG + 3):
        cur = g < NG
        p1 = g - 1
        p2 = g - 2
        p3 = g - 3
        has1 = 1 <= g <= NG
        has2 = 2 <= g <= NG + 1
        has3 = 3 <= g <= NG + 2
        if has3:
            acc(4, [(C7r, UU[0], "UU0"), (nC7r, UU[1], "UU1"), (nS7r, UU[2], "UU2"), (nS7r, UU[3], "UU3")], npart=128)
            ys, ysk = yst[p3 % 2], "ffyst%d" % (p3 % 2)
            pc0 = 4 * p3
            P.add("act", lambda e: e.activation(out=ys[:], in_=C.psb[4][0:64, :].rearrange("p (c k) -> p c k", c=4), func=AF.Copy), r=["psb4"], w=[ysk])
            P.add("pool", lambda e: e.dma_start(out=scr["y"][pc0:pc0 + 4, :].rearrange("c (h l) -> h c l", l=128), in_=ys[:]), r=[ysk], w=[("s_y", p3)], dma=True)
        if has2:
            M2 = MM2[p2 % 2]
            for c in range(4):
                b = 6 + c // 2
                o_ap = psall[:, b * 512 + (c % 2) * 256: b * 512 + (c % 2) * 256 + 256]
                for i, (mi, tab) in enumerate(((0, F5a), (1, nF5a), (2, F5b), (3, F5b))):
                    P.add("pe", lambda e: e.matmul(o_ap, lhsT=M2[mi][:, c, :], rhs=tab, start=(i == 0), stop=(i == 3)), r=["MM%d_%d" % (p2 % 2, mi), "fftr"], w=["psb%d" % b])
            dd, ddk = pv(6)
            prods("dve", UU, ["UU%d" % n for n in range(4)], dd[:, :, 0, :], dd[:, :, 1, :], ddk)
        if cur:
            T_ = TS[g % 2]
            tk_ = lambda q, n: "T%d%s%d" % (g % 2, q, n)
            if g == 0:
                ffload(0)
            s1(g, 0, 0)
            s1(g, 2, 2)
            az, azk = pv(0)
            prods("dve", [T_[("z", n)] for n in range(4)], [tk_("z", n) for n in range(4)], az[:, :, 0, :], az[:, :, 1, :], azk)
            ab, abk = pv(2)
            P.add("act", lambda e: e.activation(out=absb[:], in_=ab, func=AF.Copy), r=abk, w=["absb"])
        if g + 1 < NG:
            ffload(g + 1)
        if has1:
            Tp = TS[p1 % 2]
            pk_ = lambda q, n: "T%d%s%d" % (p1 % 2, q, n)
            tz = lambda n: (Tp[("z", n)], pk_("z", n))
            tf = lambda n: (Tp[("f", n)], pk_("f", n))
            tb = lambda n: (Tp[("b", n)], pk_("b", n))
            acc(4, [(C3r,) + tz(0), (C3r,) + tz(1), (S3r,) + tz(2), (nS3r,) + tz(3)])
            acc(5, [(C3r,) + tz(2), (nC3r,) + tz(3), (nS3r,) + tz(0), (nS3r,) + tz(1)])
            acc(6, [(C3r,) + tf(0), (C3r,) + tf(1), (S3r,) + tf(2), (nS3r,) + tf(3), (C3r,) + tb(0), (C3r,) + tb(1), (S3r,) + tb(2), (nS3r,) + tb(3)])
            acc(7, [(C3r,) + tf(2), (nC3r,) + tf(3), (nS3r,) + tf(0), (nS3r,) + tf(1), (nC3r,) + tb(2), (C3r,) + tb(3), (S3r,) + tb(0), (S3r,) + tb(1)])
            P.add("act", lambda e: e.activation(out=kre[:], in_=C.psb[6][:].rearrange("p (c k) -> p c k", c=4), func=AF.Copy), r=["psb6"], w=["kre"])
            P.add("act", lambda e: e.activation(out=kim[:], in_=C.psb[7][:].rearrange("p (c k) -> p c k", c=4), func=AF.Copy), r=["psb7"], w=["kim"])
        if cur:
            s1(g, 1, 0)
            prods("pool", [T_[("b", n)] for n in range(4)], [tk_("b", n) for n in range(4)], absb[:, :, 0, :], absb[:, :, 1, :], ["absb"])
        if has1:
            xre = C.psb[4][:].rearrange("p (c k) -> p c k", c=4)
            xim = C.psb[5][:].rearrange("p (c k) -> p c k", c=4)
            M1 = MM2[p1 % 2]
            for n, (a, ak, k_, kk) in enumerate(((xre, "psb4", kre, "kre"), (xim, "psb5", kim, "kim"), (xre, "psb4", kim, "kim"), (xim, "psb5", kre, "kre"))):
                P.add("dve", lambda e: e.tensor_tensor(out=M1[n][:], in0=a, in1=k_[:], op=ALU.mult), r=[ak, kk], w=["MM%d_%d" % (p1 % 2, n)])
        if cur:
            af, afk = pv(0)
            prods("dve", [T_[("f", n)] for n in range(4)], [tk_("f", n) for n in range(4)], af[:, :, 0, :], af[:, :, 1, :], afk)
    P.barrier()

    A32.reset()
    stg = [A32.get([128, CH + 2]) for i in range(3)]
    for q4 in range(4):
        qs_ = slice(q4 * CH, (q4 + 1) * CH)
        sg, sk = nstg()
        P.add("sp", lambda e: e.dma_start(out=sg[:, 0:CH], in_=scr["y"][:, qs_]), w=[sk], dma=True)
        P.add("act", lambda e: e.activation(out=sg[:, 0:CH], in_=sg[:, 0:CH], func=AF.Copy, scale=par[:, 41:42]), r=[sk, "par"], w=[sk])
        P.add("dve", lambda e: e.scalar_tensor_tensor(out=sg[:, 0:CH], in0=big1[:, qs_], scalar=hp[:, 14:15], in1=sg[:, 0:CH], op0=ALU.mult, op1=ALU.add),
              r=[sk, "hyp"], w=[sk])
        P.add("pool", lambda e: e.tensor_tensor(out=sg[:, 0:CH], in0=sg[:, 0:CH], in1=big2[:, qs_], op=ALU.mult), r=[sk], w=[sk])
        store(C, o_y[:, qs_], sg[:, 0:CH], [sk])


_PROGS = {}


def _prog(name):
    if name not in _PROGS:
        if name == "A0":
            _PROGS[name] = build_A0()
        elif name == "B":
            _PROGS[name] = build_B(True, True)
        elif name == "C1":
            _PROGS[name] = build_C(True)
        elif name == "C0":
            _PROGS[name] = build_C(False)
    return _PROGS[name]


def _run(name, maps):
    C = _prog(name)
    res = run_bass_kernel_spmd(C.nc, maps, core_ids=list(range(NC)))
    return res.results


def _win(mT, core):
    o = np.zeros((mT.shape[0], TPC + 2), np.float32)
    lo = core * TPC - 1
    hi = core * TPC + TPC + 1
    slo, shi = max(lo, 0), min(hi, L)
    o[:, slo - lo: slo - lo + (shi - slo)] = mT[:, slo:shi]
    return o


def _f16(v):
    return np.ascontiguousarray(np.asarray(v, np.float32).reshape(16, 128).T)


def kernel(**inp):
    inp = {k: np.asarray(v) for k, v in inp.items()}
    cos, sin = host_rope()
    cstA = host_cstA()
    cstC = np.full((128, 128), 1.0 / 1024, np.float32)
    cstB = np.ones((128, 128), np.float32)
    xT = np.ascontiguousarray(inp["x"][0].T)

    def qkg(l):
        return np.ascontiguousarray(np.stack([inp["q_norm_g"][l], inp["k_norm_g"][l]], 1).astype(np.float32))

    def tsl(i):
        return slice(i * TPC, (i + 1) * TPC)

    maps = []
    for i in range(NC):
        maps.append({"xT": np.ascontiguousarray(xT[:, tsl(i)]), "w_in": inp["w_in"][0], "lng": _f16(inp["ln_in_g"]), "lnb": _f16(inp["ln_in_b"]),
                     "qkg": qkg(0), "cos": np.ascontiguousarray(cos[:, tsl(i)]), "sin": np.ascontiguousarray(sin[:, tsl(i)]), "cstA": cstA})
    r = _run("A0", maps)
    cat = lambda key: np.concatenate([r[i][key] for i in range(NC)], axis=1)
    qT, kT, hyT, xcur = cat("qT"), cat("kT"), cat("hyT"), cat("x0T")
    v = np.concatenate([r[i]["v"] for i in range(NC)], axis=0)
    out = None
    for l in range(2):
        maps = []
        for j in range(NC):
            kv = j // 4
            m = {"cstB": cstB, "qT": np.ascontiguousarray(qT[j * 128:(j + 1) * 128]), "kT": np.ascontiguousarray(kT[kv * 128:(kv + 1) * 128]),
                 "v": np.ascontiguousarray(v[:, kv * 128:(kv + 1) * 128])}
            m.update(host_hy_inputs(inp, l, j, hyT))
            rw = DFF // NC
            m.update({"wo_f32": np.ascontiguousarray(inp["w_out"][l][j * 256:(j + 1) * 256]), "wu_f32": np.ascontiguousarray(inp["w_up"][l][j * 256:(j + 1) * 256]),
                      "wd_f32": np.ascontiguousarray(inp["w_down"][l][j * rw:(j + 1) * rw]), "wi_f32": np.ascontiguousarray(inp["w_in"][1][j * 256:(j + 1) * 256])})
            maps.append(m)
        r = _run("B", maps)
        wbf = {nm: np.concatenate([r[j][nm + "_bf"] for j in range(NC)], axis=0) for nm in ("wo", "wu", "wd", "wi")}
        aT = np.concatenate([r[j]["aT"] for j in range(NC)], axis=0)
        yT = np.concatenate([r[j]["yT"] for j in range(NC)], axis=0)
        maps = []
        for i in range(NC):
            m = {"aT": _win(aT, i), "yT": _win(yT, i), "xT": _win(xcur, i), "w_out": wbf["wo"], "w_up": wbf["wu"], "w_down": wbf["wd"],
                 "prmC": host_prmC(inp, l, i), "cstA": cstA, "cstC": cstC}
            if l == 0:
                m.update({"w_in": wbf["wi"], "qkg": qkg(1), "cos": np.ascontiguousarray(cos[:, tsl(i)]), "sin": np.ascontiguousarray(sin[:, tsl(i)])})
            maps.append(m)
        r = _run("C1" if l == 0 else "C0", maps)
        xcur = np.concatenate([r[i]["x2T"] for i in range(NC)], axis=1)
        if l == 0:
            qT, kT, hyT = cat("qT"), cat("kT"), cat("hyT")
            v = np.concatenate([r[i]["v"] for i in range(NC)], axis=0)
    out = np.ascontiguousarray(xcur.T)[None].astype(np.float32)
    return out
```

```python
import math
import numpy as np
from contextlib import ExitStack
import concourse.bass as bass
import concourse.mybir as mybir
from concourse.bass_utils import run_bass_kernel_spmd

F32 = mybir.dt.float32
F32R = mybir.dt.float32r
BF16 = mybir.dt.bfloat16
AF = mybir.ActivationFunctionType
ALU = mybir.AluOpType
AX = mybir.AxisListType

D = 2048
L = 8192
NC = 8
TPC = L // NC
HT = 512
DIN = 4608
DFF = 5504
NFF = DFF // 128
ALPHA = (2.0 * 2) ** 0.25
LN_EPS = 1e-5
RMS_EPS = 1e-6


class Op:
    __slots__ = ("eng", "fn", "deps", "idx", "dma", "sig", "slot", "val", "j")


class _Rec:
    def __init__(self):
        self.call = None

    def __getattr__(self, name):
        def f(*args, **kwargs):
            self.call = (name, args, kwargs)
            return None
        return f


class Prog:
    ENG = ["pe", "dve", "act", "pool", "sp"]

    def __init__(self, nc, stack, ndma_slots=8):
        self.nc = nc
        self.stack = stack
        self.streams = {e: [] for e in self.ENG}
        self.lastw = {}
        self.readers = {}
        self.dma_count = {e: 0 for e in self.ENG}
        self.R = ndma_slots
        self.live_dma = []

    def sb(self, name, shape, dtype=F32):
        if not hasattr(self, "_cache"):
            self._cache = {}
        if name in self._cache:
            return self._cache[name]
        t = self.stack.enter_context(self.nc.sbuf_tensor(name, list(shape), dtype))
        self._cache[name] = t
        return t

    def ps(self, name, shape, dtype=F32):
        return self.stack.enter_context(self.nc.psum_tensor(name, list(shape), dtype))

    def add(self, eng, fn, r=(), w=(), dma=False):
        op = Op()
        op.eng = eng
        if fn is not None:
            rec = _Rec()
            fn(rec)
            name, args, kwargs = rec.call
            fn = (lambda e, name=name, args=args, kwargs=kwargs: getattr(e, name)(*args, **kwargs))
        op.fn = fn
        op.dma = dma
        op.sig = False
        deps = {}

        def adddep(d, kind):
            if d is op:
                return
            if deps.get(d) != "raw":
                deps[d] = kind

        for k in r:
            lw = self.lastw.get(k)
            if lw is not None:
                adddep(lw, "raw")
        for k in w:
            lw = self.lastw.get(k)
            if lw is not None:
                adddep(lw, "waw")
            rd = self.readers.get(k)
            if rd:
                for d in rd[0].values():
                    adddep(d, "war")
                for d in rd[1]:
                    adddep(d, "war")
        for k in r:
            rd = self.readers.setdefault(k, ({}, []))
            if dma:
                rd[1].append(op)
            else:
                rd[0][eng] = op
        for k in w:
            self.lastw[k] = op
            self.readers[k] = ({}, [])
        final = []
        best = {}
        for d, kind in deps.items():
            if d.dma:
                final.append(d)
                continue
            if (not dma) and d.eng == eng:
                if eng == "pe" or kind != "raw":
                    continue
            b = best.get(d.eng)
            if b is None or d.idx > b.idx:
                best[d.eng] = d
        final.extend(best.values())
        for d in final:
            d.sig = True
        op.deps = final
        op.idx = len(self.streams[eng])
        if dma:
            op.j = self.dma_count[eng]
            self.dma_count[eng] += 1
            op.slot = op.j % self.R
            op.val = 16 * (op.j // self.R + 1)
            op.sig = True
            self.live_dma.append(op)
        self.streams[eng].append(op)
        return op

    def barrier(self):
        lasts = {e: (self.streams[e][-1] if self.streams[e] else None) for e in self.ENG}
        dmas = list(self.live_dma)
        for e in self.ENG:
            op = Op()
            op.eng = e
            op.fn = None
            op.dma = False
            op.sig = False
            deps = list(dmas)
            for e2, lo in lasts.items():
                if lo is None:
                    continue
                if lo.dma:
                    for cand in reversed(self.streams[e2]):
                        if not cand.dma and cand.fn is not None:
                            lo = cand
                            break
                    else:
                        continue
                if lo.fn is None:
                    for cand in reversed(self.streams[e2]):
                        if not cand.dma and cand.fn is not None:
                            lo = cand
                            break
                    else:
                        continue
                if e2 != e or True:
                    deps.append(lo)
            for d in deps:
                d.sig = True
            op.deps = deps
            op.idx = len(self.streams[e])
            self.streams[e].append(op)
        self.lastw = {}
        self.readers = {}
        self.live_dma = []

    def emit(self):
        nc = self.nc
        st = self.stack
        sem_e = {e: st.enter_context(nc.semaphore("s_" + e)) for e in self.ENG}
        dsem = {}
        for e in self.ENG:
            if self.dma_count[e]:
                dsem[e] = [st.enter_context(nc.semaphore("d_%s%d" % (e, i))) for i in range(self.R)]
        for e in self.ENG:
            c = 0
            for op in self.streams[e]:
                if not op.dma and op.fn is not None:
                    if op.sig:
                        c += 1
                        op.val = c
        block = st.enter_context(nc.Block())
        hooks = {"pe": block.tensor, "dve": block.vector, "act": block.scalar,
                 "pool": block.gpsimd, "sp": block.sync}

        def run(e):
            def body(eng):
                seen = {}
                for op in self.streams[e]:
                    waits = []
                    for d in op.deps:
                        if d.dma:
                            waits.append((dsem[d.eng][d.slot], d.val))
                        else:
                            waits.append((sem_e[d.eng], d.val))
                    if op.dma and op.j >= self.R:
                        waits.append((dsem[e][op.slot], op.val - 16))
                    for s, v in waits:
                        key = id(s)
                        if seen.get(key, 0) < v:
                            eng.wait_ge(s, v)
                            seen[key] = v
                    if op.fn is None:
                        continue
                    ins = op.fn(eng)
                    if op.dma:
                        ins.then_inc(dsem[e][op.slot], 16)
                    elif op.sig:
                        ins.then_inc(sem_e[e], 1)
            hooks[e](body)

        for e in self.ENG:
            if self.streams[e]:
                run(e)


class Ctx:
    pass


class Arena:
    def __init__(self, t, ncols):
        self.t = t
        self.n = ncols
        self.off = 0

    def reset(self):
        self.off = 0

    def get(self, shape):
        npart = shape[0]
        n = 1
        for d in shape[1:]:
            n *= d
        assert self.off + n <= self.n, ("arena overflow", self.off, n, self.n)
        ap = self.t[0:npart, self.off:self.off + n]
        self.off += n
        if len(shape) == 3:
            ap = ap.rearrange("p (a b) -> p a b", a=shape[1])
        elif len(shape) == 4:
            ap = ap.rearrange("p (a b c) -> p a b c", a=shape[1], b=shape[2])
        return ap


def new_ctx(name):
    nc = bass.Bass("TRN2", target_bir_lowering=False)
    C = Ctx()
    C.nc = nc
    C.stack = ExitStack()
    C.P = Prog(nc, C.stack)
    C.ins = {}
    C.outs = {}
    C.uid = 0
    return C


def din(C, name, shape, dtype=F32):
    t = C.nc.dram_tensor(name, list(shape), dtype, kind="ExternalInput").ap()
    C.ins[name] = t
    return t


def dout(C, name, shape, dtype=F32):
    t = C.nc.dram_tensor(name, list(shape), dtype, kind="ExternalOutput").ap()
    C.outs[name] = t
    return t


def setup_common(C):
    P = C.P
    C.psall = P.ps("psall", [128, 4096])
    C.psb = [C.psall[:, i * 512:(i + 1) * 512] for i in range(8)]
    C.outkeys = []


def load_const(C, name, dram_ap, shape, dtype=F32, q="sp", key=None):
    P = C.P
    t = P.sb(name, shape, dtype)
    P.add(q, lambda e: e.dma_start(out=t[:], in_=dram_ap), w=[key or name], dma=True)
    return t


def store(C, dram_ap, sb_ap, rkeys, q="sp"):
    P = C.P
    C.uid += 1
    k = "out%d" % C.uid
    C.outkeys.append(k)
    P.add(q, lambda e: e.dma_start(out=dram_ap, in_=sb_ap), r=rkeys, w=[k], dma=True)


def finish(C):
    P = C.P
    P.add("sp", None, r=C.outkeys)
    P.emit()
    C.stack.close()


def emit_ln(C, src, srckey, T, gt, bt, dst32, dstbf, dstkey, tag, alpha=None, add=None, addkey=None, pkeys=("lnp", "lnp2"), c0=0):
    P = C.P
    nb = (T + 511) // 512
    sq = [P.sb("ln_sqb%d" % (i,), [128, 512], BF16) for i in range(2)]
    m2 = P.sb("ln_m2", [128, 512])
    var = P.sb("ln_var", [128, 512])
    rstd = P.sb("ln_rstd", [128, 512])
    nmr = P.sb("ln_nmr", [128, 512])
    u = [P.sb("ln_u%d" % (i,), [128, 512]) for i in range(2)]
    tag = ""
    for b in range(nb):
        t0 = b * 512
        tw = min(512, T - t0)
        sl = slice(c0 + t0, c0 + t0 + tw)
        dl = slice(t0, t0 + tw) if c0 else sl
        pm = C.psb[6]
        pq = C.psb[7]
        for c in range(16):
            if add is not None:
                P.add("dve", lambda e, c=c, sl=sl: e.scalar_tensor_tensor(
                    out=src[:, c, sl], in0=src[:, c, sl], scalar=float(alpha), in1=add[:, c, sl],
                    op0=ALU.mult, op1=ALU.add), r=[(srckey, c), (addkey, c)], w=[(srckey, c)])
            s = sq[c % 2]
            P.add("act", lambda e, c=c, s=s, sl=sl, tw=tw: e.activation(out=s[:, 0:tw], in_=src[:, c, sl], func=AF.Square),
                  r=[(srckey, c)], w=["lnsq%s%d" % (tag, c % 2)])
            P.add("pe", lambda e, c=c, sl=sl, tw=tw: e.matmul(pm[:, 0:tw], lhsT=C.onesD[:], rhs=src[:, c, sl], start=(c == 0), stop=(c == 15)),
                  r=[(srckey, c), "onesD"], w=["psb6"])
            P.add("pe", lambda e, c=c, s=s, tw=tw: e.matmul(pq[:, 0:tw], lhsT=C.onesDb[:], rhs=s[:, 0:tw], start=(c == 0), stop=(c == 15)),
                  r=["lnsq%s%d" % (tag, c % 2), "onesD"], w=["psb7"])
        P.add("act", lambda e, tw=tw: e.activation(out=m2[:, 0:tw], in_=pm[:, 0:tw], func=AF.Square), r=["psb6"], w=["lnm2" + tag])
        P.add("dve", lambda e, tw=tw: e.tensor_tensor(out=var[:, 0:tw], in0=pq[:, 0:tw], in1=m2[:, 0:tw], op=ALU.subtract),
              r=["psb7", "lnm2" + tag], w=["lnvar" + tag])
        P.add("act", lambda e, tw=tw: e.activation(out=var[:, 0:tw], in_=var[:, 0:tw], func=AF.Sqrt, bias=C.eps_ln[:, 0:1]),
              r=["lnvar" + tag, "eps"], w=["lnvar" + tag])
        P.add("dve", lambda e, tw=tw: e.reciprocal(out=rstd[:, 0:tw], in_=var[:, 0:tw]), r=["lnvar" + tag], w=["lnrstd" + tag])
        P.add("dve", lambda e, tw=tw: e.scalar_tensor_tensor(out=nmr[:, 0:tw], in0=pm[:, 0:tw], scalar=-1.0, in1=rstd[:, 0:tw],
                                                              op0=ALU.mult, op1=ALU.mult),
              r=["psb6", "lnrstd" + tag], w=["lnnmr" + tag])
        for c in range(16):
            uu = u[c % 2]
            uk = "lnu%s%d" % (tag, c % 2)
            P.add("dve", lambda e, c=c, uu=uu, sl=sl, tw=tw: e.scalar_tensor_tensor(
                out=uu[:, 0:tw], in0=src[:, c, sl], scalar=gt[:, c:c + 1], in1=rstd[:, 0:tw], op0=ALU.mult, op1=ALU.mult),
                r=[(srckey, c), "lnrstd" + tag, pkeys[0]], w=[uk])
            P.add("dve", lambda e, c=c, uu=uu, tw=tw: e.scalar_tensor_tensor(
                out=uu[:, 0:tw], in0=nmr[:, 0:tw], scalar=gt[:, c:c + 1], in1=uu[:, 0:tw], op0=ALU.mult, op1=ALU.add),
                r=[uk, "lnnmr" + tag, pkeys[0]], w=[uk])
            if dst32 is not None:
                P.add("act", lambda e, c=c, uu=uu, dl=dl, tw=tw: e.activation(out=dst32[:, c, dl], in_=uu[:, 0:tw], func=AF.Identity, bias=bt[:, c:c + 1]),
                      r=[uk, pkeys[1]], w=[(dstkey + "32", c)])
            if dstbf is not None:
                P.add("act", lambda e, c=c, uu=uu, dl=dl, tw=tw: e.activation(out=dstbf[:, c, dl], in_=uu[:, 0:tw], func=AF.Identity, bias=bt[:, c:c + 1]),
                      r=[uk, pkeys[1]], w=[(dstkey + "bf", c)])


def emit_inproj(C, xb, xbkey, w_in, qg_t, kg_t, cos_t, sin_t, tcol0, o_q, o_k, o_v, o_hy, ocol0, NT=1):
    P = C.P
    wv = w_in.rearrange("(c p) n -> p c n", p=128)
    pend = []
    for og in range(9):
        wb = C.wbuf[og % len(C.wbuf)]
        wk = "wbuf%d" % (og % len(C.wbuf))
        P.add(getattr(C, "wq", "pool"), lambda e, og=og, wb=wb: e.dma_start(out=wb[:], in_=wv[:, :, og * 512:(og + 1) * 512]), w=[wk], dma=True)
        for j, tb in [(j_, t_) for j_ in range(4) for t_ in range(NT)]:
            xbt = xb[:, :, tb * 512:(tb + 1) * 512]
            oc0 = ocol0 + tb * 512
            tc0 = tcol0 + tb * 512
            ch = og * 4 + j
            if ch in (10, 11):
                if ch == 11:
                    continue
                for tt in range(4):
                    pb = C.psb[C.rr % 4]
                    pk = "psb%d" % (C.rr % 4)
                    C.rr += 1
                    for c in range(16):
                        P.add("pe", lambda e, c=c, tt=tt, pb=pb, wb=wb: e.matmul(
                            pb[:, 0:256], lhsT=xbt[:, c, tt * 128:(tt + 1) * 128], rhs=wb[:, c, 256:512],
                            start=(c == 0), stop=(c == 15)), r=[(xbkey, c), wk], w=[pk])
                    sg = C.stg[C.sr % 4]
                    sk = "stg%d" % (C.sr % 4)
                    C.sr += 1
                    P.add("act", lambda e, pb=pb, sg=sg: e.activation(out=sg[:, 0:256], in_=pb[:, 0:256], func=AF.Copy), r=[pk], w=[sk])
                    store(C, o_v[oc0 + tt * 128: oc0 + (tt + 1) * 128, :], sg[:, 0:256], [sk])
                continue
            pb = C.psb[C.rr % 4]
            pk = "psb%d" % (C.rr % 4)
            C.rr += 1
            for c in range(16):
                P.add("pe", lambda e, c=c, j=j, pb=pb, wb=wb: e.matmul(
                    pb[:], lhsT=wb[:, c, j * 128:(j + 1) * 128], rhs=xbt[:, c, :], start=(c == 0), stop=(c == 15)),
                    r=[(xbkey, c), wk], w=[pk])
            for t_ in pend:
                t_()
            pend.clear()
            sg = C.stg[C.sr % 4]
            sk = "stg%d" % (C.sr % 4)
            C.sr += 1
            if ch >= 12:
                eng = "act" if (ch % 2 == 0) else "dve"
                if eng == "act":
                    P.add("act", lambda e, pb=pb, sg=sg: e.activation(out=sg[:], in_=pb[:], func=AF.Copy), r=[pk], w=[sk])
                else:
                    P.add("dve", lambda e, pb=pb, sg=sg: e.tensor_copy(out=sg[:], in_=pb[:]), r=[pk], w=[sk])
                hrow = (ch - 12) * 128
                store(C, o_hy[hrow:hrow + 128, oc0:oc0 + 512], sg[:], [sk])
                continue
            gt = qg_t if ch < 8 else kg_t
            i2 = C.qr % 2
            C.qr += 1
            qg = C.qk_qg[i2]
            sq = C.qk_sq[i2]
            t1 = C.qk_t1[i2]
            t2 = C.qk_t2[i2]
            kq, ks, k1, k2 = "qkqg%d" % i2, "qksq%d" % i2, "qkt1%d" % i2, "qkt2%d" % i2
            pss = C.psb[4 + i2]
            pks = "psb%d" % (4 + i2)
            psr = C.psb[6 + i2]
            pkr = "psb%d" % (6 + i2)
            P.add("act", lambda e, pb=pb, qg=qg, gt=gt: e.activation(out=qg[:], in_=pb[:], func=AF.Identity, scale=gt[:, 0:1]), r=[pk, "qkg"], w=[kq])
            P.add("act", lambda e, pb=pb, sq=sq: e.activation(out=sq[:], in_=pb[:], func=AF.Square), r=[pk], w=[ks])
            def tail(tc0=tc0, oc0=oc0, ch=ch, sq=sq, qg=qg, t1=t1, t2=t2, sg=sg, sk=sk, pss=pss, pks=pks, psr=psr, pkr=pkr, kq=kq, ks=ks, k1=k1, k2=k2):
                P.add("pe", lambda e, sq=sq, pss=pss: e.matmul(pss[:], lhsT=C.onesH[:], rhs=sq[:], start=True, stop=True), r=[ks, "onesH"], w=[pks])
                P.add("pe", lambda e, qg=qg, psr=psr: e.matmul(psr[:], lhsT=C.rotm[:], rhs=qg[:], start=True, stop=True), r=[kq, "rotm"], w=[pkr])
                tc = slice(tc0, tc0 + 512)
                P.add("dve", lambda e, qg=qg, t1=t1, tc=tc: e.tensor_tensor(out=t1[:], in0=qg[:].bitcast(F32), in1=cos_t[:, tc], op=ALU.mult), r=[kq, "rope"], w=[k1])
                P.add("dve", lambda e, psr=psr, t2=t2, tc=tc: e.tensor_tensor(out=t2[:], in0=psr[:], in1=sin_t[:, tc], op=ALU.mult), r=[pkr, "rope2"], w=[k2])
                P.add("pool", lambda e, t1=t1, t2=t2: e.tensor_tensor(out=t1[:], in0=t1[:], in1=t2[:], op=ALU.add), r=[k1, k2], w=[k1])
                P.add("act", lambda e, pss=pss, t2=t2: e.activation(out=t2[:], in_=pss[:], func=AF.Sqrt, bias=C.eps_rms[:, 0:1]), r=[pks, "eps", k1], w=[k2])
                P.add("dve", lambda e, t2=t2: e.reciprocal(out=t2[:], in_=t2[:]), r=[k2], w=[k2])
                P.add("dve", lambda e, t1=t1, t2=t2, sg=sg: e.tensor_tensor(out=sg[:], in0=t1[:], in1=t2[:], op=ALU.mult), r=[k1, k2], w=[sk])
                if ch < 8:
                    store(C, o_q[ch * 128:(ch + 1) * 128, oc0:oc0 + 512], sg[:], [sk])
                else:
                    store(C, o_k[(ch - 8) * 128:(ch - 7) * 128, oc0:oc0 + 512], sg[:], [sk])
            pend.append(tail)
    for t_ in pend:
        t_()
    pend.clear()


def setup_inproj(C, small=False):
    P = C.P
    C.wbuf = [P.sb("wbuf%d" % i, [128, 16, 512], BF16) for i in range(2 if small else 3)]
    C.stg = [P.sb("stg%d" % i, [128, 512]) for i in range(4)]
    C.qk_qg = [P.sb("qkqg%d" % i, [128, 512], F32R) for i in range(2)]
    C.qk_sq = [P.sb("qksq%d" % i, [128, 512], F32R) for i in range(2)]
    if small:
        C.qk_t1 = [P.sb("ct%d" % i, [128, 512]) for i in range(2)]
        C.qk_t2 = [P.sb("ct%d" % (2 + i), [128, 512]) for i in range(2)]
    else:
        C.qk_t1 = [P.sb("qkt1%d" % i, [128, 512]) for i in range(2)]
        C.qk_t2 = [P.sb("qkt2%d" % i, [128, 512]) for i in range(2)]
    C.rr = 0
    C.sr = 0
    C.qr = 0


def setup_consts_A(C):
    P = C.P
    cst = din(C, "cstA", [128, 3 * 128 + 2])
    raw = load_const(C, "cstA_sb", cst, [128, 3 * 128 + 2])
    C.onesD = P.sb("onesD", [128, 128])
    C.onesH = P.sb("onesH", [128, 128], F32R)
    C.rotm = P.sb("rotm", [128, 128], F32R)
    C.eps_ln = P.sb("eps_ln", [128, 1])
    C.eps_rms = P.sb("eps_rms", [128, 1])
    P.add("dve", lambda e: e.tensor_copy(out=C.onesD[:], in_=raw[:, 0:128]), r=["cstA_sb"], w=["onesD"])
    C.onesDb = P.sb("onesDb", [128, 128], BF16)
    P.add("dve", lambda e: e.tensor_copy(out=C.onesDb[:], in_=raw[:, 0:128]), r=["cstA_sb", "onesD"], w=["onesD"])
    P.add("dve", lambda e: e.tensor_copy(out=C.onesH[:], in_=raw[:, 128:256]), r=["cstA_sb"], w=["onesH"])
    P.add("dve", lambda e: e.tensor_copy(out=C.rotm[:], in_=raw[:, 256:384]), r=["cstA_sb"], w=["rotm"])
    P.add("dve", lambda e: e.tensor_copy(out=C.eps_ln[:], in_=raw[:, 384:385]), r=["cstA_sb"], w=["eps"])
    P.add("dve", lambda e: e.tensor_copy(out=C.eps_rms[:], in_=raw[:, 385:386]), r=["cstA_sb"], w=["eps"])


def host_cstA():
    c = np.zeros((128, 386), np.float32)
    c[:, 0:128] = 1.0 / D
    c[:, 128:256] = 1.0 / 128
    R = np.zeros((128, 128), np.float32)
    for base in (0, 64):
        for m in range(32):
            R[base + m + 32, base + m] = -1.0
            R[base + m, base + m + 32] = 1.0
    c[:, 256:384] = R
    c[:, 384] = LN_EPS
    c[:, 385] = RMS_EPS
    return c


def host_rope():
    t = np.arange(L)
    row = (t // 64).astype(np.float32)
    col = (t % 64).astype(np.float32)
    inv = (10000.0 ** (-np.arange(0, 64, 2, dtype=np.float32) / 64)).astype(np.float32)
    ang = np.zeros((128, L), np.float32)
    for d in range(128):
        if d < 64:
            ang[d] = row * inv[d % 32]
        else:
            ang[d] = col * inv[(d - 64) % 32]
    return np.cos(ang).astype(np.float32), np.sin(ang).astype(np.float32)


def build_A0():
    C = new_ctx("A0")
    P = C.P
    xT = din(C, "xT", [D, TPC])
    w_in = din(C, "w_in", [D, DIN])
    lng = din(C, "lng", [128, 16])
    lnb = din(C, "lnb", [128, 16])
    qkg = din(C, "qkg", [128, 2])
    cosd = din(C, "cos", [128, TPC])
    sind = din(C, "sin", [128, TPC])
    o_q = dout(C, "qT", [1024, TPC])
    o_k = dout(C, "kT", [256, TPC])
    o_v = dout(C, "v", [TPC, 256])
    o_hy = dout(C, "hyT", [3072, TPC])
    o_x = dout(C, "x0T", [D, TPC])
    setup_common(C)
    setup_consts_A(C)
    setup_inproj(C)
    gt = load_const(C, "lng_sb", lng, [128, 16], key="lnp")
    bt = load_const(C, "lnb_sb", lnb, [128, 16], key="lnp2")
    qk = load_const(C, "qkg_sb", qkg, [128, 2], key="qkg")
    cos_t = load_const(C, "cos_sb", cosd, [128, TPC], key="rope")
    sin_t = load_const(C, "sin_sb", sind, [128, TPC], key="rope2")
    xs = P.sb("xs", [128, 16, HT])
    x32 = P.sb("x32", [128, 16, HT])
    xb = P.sb("xb", [128, 16, TPC], BF16)
    xv = xT.rearrange("(c p) t -> p c t", p=128)
    xov = o_x.rearrange("(c p) t -> p c t", p=128)
    for h in range(2):
        for c4 in range(4):
            P.add("sp", lambda e, h=h, c4=c4: e.dma_start(out=xs[:, c4 * 4:(c4 + 1) * 4, :], in_=xv[:, c4 * 4:(c4 + 1) * 4, h * HT:(h + 1) * HT]),
                  w=[("xs", c) for c in range(c4 * 4, c4 * 4 + 4)], dma=True)
        emit_ln(C, xs, "xs", HT, gt, bt, x32, xb[:, :, h * HT:(h + 1) * HT], "x", "a")
        for c4 in range(4):
            store(C, xov[:, c4 * 4:(c4 + 1) * 4, h * HT:(h + 1) * HT], x32[:, c4 * 4:(c4 + 1) * 4, :], [("x32", c) for c in range(c4 * 4, c4 * 4 + 4)])
    emit_inproj(C, xb, "xbf", w_in, qk[:, 0:1], qk[:, 1:2], cos_t, sin_t, 0, o_q, o_k, o_v, o_hy, 0, NT=2)
    finish(C)
    return C


WIN = HT + 2
PC_MIXG, PC_BOUT, PC_L1G, PC_L1B, PC_L2G, PC_L2B, PC_BDN = 0, 16, 32, 48, 64, 80, 96
PC_BG, PC_BV, PC_CW, PC_CB, PC_MASK, PC_N = 112, 155, 198, 327, 370, 374


def host_prmC(inp, l, core):
    p = np.zeros((128, PC_N), np.float32)
    f16 = lambda v: np.ascontiguousarray(v.reshape(16, 128).T)
    p[:, PC_MIXG:PC_MIXG + 16] = f16(np.concatenate([inp["att_out_g"][l], inp["hy_out_g"][l]]))
    p[:, PC_BOUT:PC_BOUT + 16] = f16(inp["b_out"][l])
    p[:, PC_L1G:PC_L1G + 16] = f16(inp["ln1_g"][l])
    p[:, PC_L1B:PC_L1B + 16] = f16(inp["ln1_b"][l])
    p[:, PC_L2G:PC_L2G + 16] = f16(inp["ln2_g"][l])
    p[:, PC_L2B:PC_L2B + 16] = f16(inp["ln2_b"][l])
    p[:, PC_BDN:PC_BDN + 16] = f16(inp["b_down"][l])
    p[:, PC_BG:PC_BG + 43] = inp["b_up"][l][:DFF].reshape(43, 128).T
    p[:, PC_BV:PC_BV + 43] = inp["b_up"][l][DFF:].reshape(43, 128).T
    for k in range(3):
        p[:, PC_CW + k * 43:PC_CW + (k + 1) * 43] = inp["ffn_conv_w"][l][k].reshape(43, 128).T
    p[:, PC_CB:PC_CB + 43] = inp["ffn_conv_b"][l].reshape(43, 128).T
    p[:, PC_MASK + 0] = 1.0 if core > 0 else 0.0
    p[:, PC_MASK + 1] = 1.0
    p[:, PC_MASK + 2] = 1.0
    p[:, PC_MASK + 3] = 1.0 if core < NC - 1 else 0.0
    return p


def emit_colstat(C, src, key, chunks, cols_list, ones, eps_ap, out, outkey, tag):
    P = C.P
    sq = [P.sb("ln_sqb%d" % i, [128, 512], BF16) for i in range(2)]
    for (cs, n, ocs) in cols_list:
        pq = C.psb[7]
        for i, c in enumerate(chunks):
            s = sq[i % 2]
            sk = "lnsq%d" % (i % 2)
            P.add("act", lambda e, c=c, s=s, cs=cs, n=n: e.activation(out=s[:, 0:n], in_=src[:, c, cs], func=AF.Square), r=[(key, c)], w=[sk])
            P.add("pe", lambda e, s=s, n=n, i=i: e.matmul(pq[:, 0:n], lhsT=ones[:], rhs=s[:, 0:n], start=(i == 0), stop=(i == len(chunks) - 1)),
                  r=[sk, "onesG"], w=["psb7"])
        P.add("act", lambda e, n=n, ocs=ocs: e.activation(out=out[:, ocs], in_=pq[:, 0:n], func=AF.Sqrt, bias=eps_ap), r=["psb7", "eps"], w=[outkey])
        P.add("dve", lambda e, ocs=ocs: e.reciprocal(out=out[:, ocs], in_=out[:, ocs]), r=[outkey], w=[outkey])


def build_C(with_inproj):
    C = new_ctx("C")
    P = C.P
    aT = din(C, "aT", [1024, TPC + 2])
    yT = din(C, "yT", [1024, TPC + 2])
    xT = din(C, "xT", [D, TPC + 2])
    w_out = din(C, "w_out", [D, D], BF16)
    w_up = din(C, "w_up", [D, 2 * DFF], BF16)
    w_down = din(C, "w_down", [DFF, D], BF16)
    prm = din(C, "prmC", [128, PC_N])
    o_x = dout(C, "x2T", [D, TPC])
    setup_common(C)
    setup_consts_A(C)
    cst2 = din(C, "cstC", [128, 128])
    onesG32 = load_const(C, "onesG", cst2, [128, 128], key="onesG32")
    C.onesG = P.sb("onesGb", [128, 128], BF16)
    P.add("dve", lambda e: e.tensor_copy(out=C.onesG[:], in_=onesG32[:]), r=["onesG32"], w=["onesG"])
    if with_inproj:
        w_in = din(C, "w_in", [D, DIN], BF16)
        C.wq = "sp"
        qkg = din(C, "qkg", [128, 2])
        cosd = din(C, "cos", [128, TPC])
        sind = din(C, "sind" if False else "sin", [128, TPC])
        o_q = dout(C, "qT", [1024, TPC])
        o_k = dout(C, "kT", [256, TPC])
        o_v = dout(C, "v", [TPC, 256])
        o_hy = dout(C, "hyT", [3072, TPC])
        qk = load_const(C, "qkg_sb", qkg, [128, 2], key="qkg")
        cos_t = P.sb("cos_sb", [128, HT])
        sin_t = P.sb("sin_sb", [128, HT])
        setup_inproj(C, small=True)
        C.wbuf.append(P.sb("wbuf2", [128, 16, 512], BF16))
    else:
        C.wbuf = [P.sb("wbuf%d" % i, [128, 16, 512], BF16) for i in range(3)]
        C.stg = [P.sb("stg%d" % i, [128, 512]) for i in range(4)]
        C.rr = 0
        C.sr = 0
    pt = load_const(C, "prm_sb", prm, [128, PC_N], key="prm")
    arena = P.sb("arena", [128, 12416])
    mix = arena[:, 0:16 * WIN].rearrange("p (c t) -> p c t", c=16)
    mixb = arena[:, 8256:8256 + 8 * WIN].bitcast(BF16).rearrange("p (c t) -> p c t", c=16)
    hT = arena[:, 0:NFF * 256].bitcast(BF16).rearrange("p (f t) -> p f t", f=NFF)
    xr = P.sb("xr", [128, 16, WIN])
    x1b = P.sb("x1b", [128, 16, WIN], BF16)
    rsa = P.sb("rsa", [128, WIN])
    rsh = P.sb("rsh", [128, WIN])
    tt = [P.sb("ct%d" % i, [128, 512]) for i in range(4)]
    gt_ = [P.sb("gt%d" % i, [128, WIN]) for i in range(2)]
    aTv = aT.rearrange("(c p) t -> p c t", p=128)
    yTv = yT.rearrange("(c p) t -> p c t", p=128)
    xTv = xT.rearrange("(c p) t -> p c t", p=128)
    xov = o_x.rearrange("(c p) t -> p c t", p=128)
    wov = w_out.rearrange("(c p) n -> p c n", p=128)
    wuv = w_up.rearrange("(c p) n -> p c n", p=128)
    wdv = w_down.rearrange("(f p) n -> p f n", p=128)
    MAIN = slice(1, 1 + HT)
    HALO = slice(0, WIN, WIN - 1)
    wq = [0]
    pre = {}

    def nextw():
        i = wq[0] % 3
        wq[0] += 1
        return C.wbuf[i], "wbuf%d" % i

    for h in range(2):
        w0 = h * HT
        for c4 in range(2):
            P.add("sp", lambda e, c4=c4, w0=w0: e.dma_start(out=mix[:, c4 * 4:(c4 + 1) * 4, :], in_=aTv[:, c4 * 4:(c4 + 1) * 4, w0:w0 + WIN]),
                  w=[("mix", c) for c in range(c4 * 4, c4 * 4 + 4)], dma=True)
            P.add("sp", lambda e, c4=c4, w0=w0: e.dma_start(out=mix[:, 8 + c4 * 4:8 + (c4 + 1) * 4, :], in_=yTv[:, c4 * 4:(c4 + 1) * 4, w0:w0 + WIN]),
                  w=[("mix", c) for c in range(8 + c4 * 4, 8 + c4 * 4 + 4)], dma=True)
        for c4 in range(4):
            P.add("sp", lambda e, c4=c4, w0=w0: e.dma_start(out=xr[:, c4 * 4:(c4 + 1) * 4, :], in_=xTv[:, c4 * 4:(c4 + 1) * 4, w0:w0 + WIN]),
                  w=[("xr", c) for c in range(c4 * 4, c4 * 4 + 4)], dma=True)
        cols = [(MAIN, HT, MAIN), (HALO, 2, HALO)]
        emit_colstat(C, mix, "mix", list(range(8)), cols, C.onesG, C.eps_rms[:, 0:1], rsa, "rsa", "a")
        emit_colstat(C, mix, "mix", list(range(8, 16)), cols, C.onesG, C.eps_rms[:, 0:1], rsh, "rsh", "h")
        for c in range(16):
            rs_, rsk = (rsa, "rsa") if c < 8 else (rsh, "rsh")
            P.add("dve", lambda e: e.scalar_tensor_tensor(out=mixb[:, c, :], in0=mix[:, c, :], scalar=pt[:, PC_MIXG + c:PC_MIXG + c + 1], in1=rs_[:],
                                                          op0=ALU.mult, op1=ALU.mult), r=[("mix", c), "prm", rsk], w=[("mixb", c)])
            P.add("act", lambda e: e.activation(out=xr[:, c, :], in_=xr[:, c, :], func=AF.Copy, scale=float(ALPHA)), r=[("xr", c)], w=[("xr", c)])
        for og in range(4):
            if h == 0 and og < 2 and ("oproj", og) in pre:
                wb, wk = pre.pop(("oproj", og))
            elif ("oproj", og) in pre:
                wb, wk = pre.pop(("oproj", og))
            else:
                wb, wk = nextw()
                P.add("sp", lambda e: e.dma_start(out=wb[:], in_=wov[:, :, og * 512:(og + 1) * 512]), w=[wk], dma=True)
            for j in range(4):
                o = og * 4 + j
                b2 = 2 * (o % 3)
                pm, pkm = C.psb[b2], "psb%d" % b2
                pz, pkz = C.psb[b2 + 1], "psb%d" % (b2 + 1)
                for c in range(16):
                    P.add("pe", lambda e: e.matmul(pm[:], lhsT=wb[:, c, j * 128:(j + 1) * 128], rhs=mixb[:, c, MAIN], start=(c == 0), stop=(c == 15)),
                          r=[("mixb", c), wk], w=[pkm])
                for c in range(16):
                    P.add("pe", lambda e: e.matmul(pz[:, 0:2], lhsT=wb[:, c, j * 128:(j + 1) * 128], rhs=mixb[:, c, HALO], start=(c == 0), stop=(c == 15)),
                          r=[("mixb", c), wk], w=[pkz])
                bo = pt[:, PC_BOUT + o:PC_BOUT + o + 1]
                P.add("dve", lambda e: e.scalar_tensor_tensor(out=xr[:, o, MAIN], in0=pm[:], scalar=bo, in1=xr[:, o, MAIN], op0=ALU.add, op1=ALU.add),
                      r=[pkm, ("xr", o), "prm"], w=[("xr", o)])
                P.add("dve", lambda e: e.scalar_tensor_tensor(out=xr[:, o, HALO], in0=pz[:, 0:2], scalar=bo, in1=xr[:, o, HALO], op0=ALU.add, op1=ALU.add),
                      r=[pkz, ("xr", o), "prm"], w=[("xr", o)])
        for g2 in range(2):
            wb, wk = nextw()
            P.add("sp", lambda e: e.dma_start(out=wb[:, :, 0:256], in_=wuv[:, :, g2 * 256:g2 * 256 + 256]), w=[wk + "g", wk], dma=True)
            P.add("sp", lambda e: e.dma_start(out=wb[:, :, 256:512], in_=wuv[:, :, DFF + g2 * 256:DFF + g2 * 256 + 256]), w=[wk + "v", wk], dma=True)
            pre[("up", g2)] = (wb, wk)
        C.lnp_g = pt[:, PC_L1G:PC_L1G + 16]
        emit_ln(C, xr, "xr", WIN, pt[:, PC_L1G:PC_L1G + 16], pt[:, PC_L1B:PC_L1B + 16], xr, x1b, "x1", "c", pkeys=("prm", "prm"))
        P.barrier()
        for g2 in range(22):
            nf = 2 if g2 < 21 else 1
            if ("up", g2) in pre:
                wb, wk = pre.pop(("up", g2))
            else:
                wb, wk = nextw()
                P.add("sp", lambda e, g2=g2, wb=wb, nf=nf: e.dma_start(out=wb[:, :, 0:nf * 128], in_=wuv[:, :, g2 * 256:g2 * 256 + nf * 128]), w=[wk + "g"], dma=True)
                P.add("sp", lambda e, g2=g2, wb=wb, nf=nf: e.dma_start(out=wb[:, :, 256:256 + nf * 128], in_=wuv[:, :, DFF + g2 * 256:DFF + g2 * 256 + nf * 128]), w=[wk + "v"], dma=True)
            for j in range(nf):
                f = g2 * 2 + j
                b3 = (f % 2) * 3
                pg, pkg = C.psb[b3], "psb%d" % b3
                pv, pkv = C.psb[b3 + 1], "psb%d" % (b3 + 1)
                pz, pkz = C.psb[b3 + 2], "psb%d" % (b3 + 2)
                for c in range(16):
                    P.add("pe", lambda e, c=c, j=j, wb=wb, pg=pg: e.matmul(pg[:], lhsT=wb[:, c, j * 128:(j + 1) * 128], rhs=x1b[:, c, MAIN], start=(c == 0), stop=(c == 15)),
                          r=[("x1bf", c), wk + "g"], w=[pkg])
                for c in range(16):
                    P.add("pe", lambda e, c=c, j=j, wb=wb, pz=pz: e.matmul(pz[:, 0:2], lhsT=wb[:, c, j * 128:(j + 1) * 128], rhs=x1b[:, c, HALO], start=(c == 0), stop=(c == 15)),
                          r=[("x1bf", c), wk + "g"], w=[pkz])
                for c in range(16):
                    P.add("pe", lambda e, c=c, j=j, wb=wb, pv=pv: e.matmul(pv[:], lhsT=wb[:, c, 256 + j * 128:256 + (j + 1) * 128], rhs=x1b[:, c, MAIN], start=(c == 0), stop=(c == 15)),
                          r=[("x1bf", c), wk + "v"], w=[pkv])
                gt = gt_[f % 2]
                gk = "gt%d" % (f % 2)
                ct = tt[f % 2]
                ck = "ct%d" % (f % 2)
                gl = tt[2 + f % 2]
                glk = "ct%d" % (2 + f % 2)
                bgp = pt[:, PC_BG + f:PC_BG + f + 1]
                bvp = pt[:, PC_BV + f:PC_BV + f + 1]
                P.add("act", lambda e, pg=pg, gt=gt, bgp=bgp: e.activation(out=gt[:, MAIN], in_=pg[:], func=AF.Identity, bias=bgp), r=[pkg, "prm"], w=[gk])
                P.add("dve", lambda e, pz=pz, gt=gt, bgp=bgp, h=h: e.scalar_tensor_tensor(out=gt[:, HALO], in0=pz[:, 0:2], scalar=bgp, in1=pt[:, PC_MASK + 2 * h:PC_MASK + 2 * h + 2],
                                                                                  op0=ALU.add, op1=ALU.mult), r=[pkz, "prm", gk], w=[gk])
                P.add("dve", lambda e, gt=gt, ct=ct, f=f: e.tensor_scalar(out=ct[:], in0=gt[:, 0:HT], scalar1=pt[:, PC_CW + f:PC_CW + f + 1], scalar2=pt[:, PC_CB + f:PC_CB + f + 1],
                                                                          op0=ALU.mult, op1=ALU.add), r=[gk, "prm"], w=[ck])
                P.add("dve", lambda e, gt=gt, ct=ct, f=f: e.scalar_tensor_tensor(out=ct[:], in0=gt[:, 1:1 + HT], scalar=pt[:, PC_CW + 43 + f:PC_CW + 43 + f + 1], in1=ct[:],
                                                                                  op0=ALU.mult, op1=ALU.add), r=[gk, ck, "prm"], w=[ck])
                P.add("dve", lambda e, gt=gt, ct=ct, f=f: e.scalar_tensor_tensor(out=ct[:], in0=gt[:, 2:2 + HT], scalar=pt[:, PC_CW + 86 + f:PC_CW + 86 + f + 1], in1=ct[:],
                                                                                 op0=ALU.mult, op1=ALU.add), r=[gk, ck, "prm"], w=[ck])
                P.add("act", lambda e, ct=ct, gl=gl: e.activation(out=gl[:], in_=ct[:], func=AF.Gelu), r=[ck], w=[glk])
                P.add("dve", lambda e, pv=pv, gl=gl, f=f, bvp=bvp: e.scalar_tensor_tensor(out=hT[:, f, :], in0=pv[:], scalar=bvp, in1=gl[:], op0=ALU.add, op1=ALU.mult),
                      r=[pkv, glk, "prm"], w=[("hT", f)])
        for o in range(16):
            wb, wk = nextw()
            wdt = wb[:].rearrange("p c n -> p (c n)")[:, 0:NFF * 128].rearrange("p (f n) -> p f n", f=NFF)
            P.add("sp", lambda e, o=o, wdt=wdt: e.dma_start(out=wdt, in_=wdv[:, :, o * 128:(o + 1) * 128]), w=[wk + "g", wk + "v", wk], dma=True)
            pd, pkd = C.psb[o % 4], "psb%d" % (o % 4)
            for f in range(NFF):
                P.add("pe", lambda e, f=f, wdt=wdt, pd=pd: e.matmul(pd[:], lhsT=wdt[:, f, :], rhs=hT[:, f, :], start=(f == 0), stop=(f == NFF - 1)),
                      r=[("hT", f), wk], w=[pkd])
            t1 = tt[o % 4]
            k1 = "ct%d" % (o % 4)
            P.add("act", lambda e, pd=pd, t1=t1, o=o: e.activation(out=t1[:], in_=pd[:], func=AF.Identity, bias=pt[:, PC_BDN + o:PC_BDN + o + 1]), r=[pkd, "prm"], w=[k1])
            P.add("dve", lambda e, t1=t1, o=o: e.scalar_tensor_tensor(out=xr[:, o, MAIN], in0=xr[:, o, MAIN], scalar=float(ALPHA), in1=t1[:], op0=ALU.mult, op1=ALU.add),
                  r=[k1, ("x132", o)], w=[("x132", o)])
        x2 = arena[:, 0:16 * HT].rearrange("p (c t) -> p c t", c=16)
        x2b = x1b[:, :, 0:HT] if with_inproj else None
        emit_ln(C, xr, "x132", HT, pt[:, PC_L2G:PC_L2G + 16], pt[:, PC_L2B:PC_L2B + 16], x2, x2b, "x2", "d", pkeys=("prm", "prm"), c0=1)
        for c4 in range(4):
            store(C, xov[:, c4 * 4:(c4 + 1) * 4, h * HT:(h + 1) * HT], x2[:, c4 * 4:(c4 + 1) * 4, :], [("x232", c) for c in range(c4 * 4, c4 * 4 + 4)])
        P.barrier()
        if with_inproj:
            P.add("sp", lambda e: e.dma_start(out=cos_t[:], in_=cosd[:, h * HT:(h + 1) * HT]), w=["rope"], dma=True)
            P.add("sp", lambda e: e.dma_start(out=sin_t[:], in_=sind[:, h * HT:(h + 1) * HT]), w=["rope2"], dma=True)
            emit_inproj(C, x2b, "x2bf", w_in, qk[:, 0:1], qk[:, 1:2], cos_t, sin_t, 0, o_q, o_k, o_v, o_hy, h * HT)
            P.barrier()
    finish(C)
    return C


def emit_attention(C, qT_d, kT_d, v_d, o_a, cstB_t):
    P = C.P
    NKB = L // 128
    import os
    NQB = int(os.environ.get('DBG_NQB', L // 512))
    AR, A32 = C.AR, C.A32
    AR.reset()
    A32.reset()
    ks = AR.get([128, L])
    vs = AR.get([128, NKB, 128])
    qblk = [AR.get([128, 512]) for i in range(2)]
    pT = [AR.get([128, 512]) for i in range(4)]
    onesK = P.sb("att_ones", [128, 128], F32R)
    stgq = [A32.get([128, 2048]) for i in range(3)]
    qst = [A32.get([128, 512]) for i in range(2)]
    rl = A32.get([128, 512])
    ob = [A32.get([128, 512]) for i in range(2)]
    P.add("dve", lambda e: e.tensor_copy(out=onesK[:], in_=cstB_t[:]), r=["cstB"], w=["att_ones"])
    vv = v_d.rearrange("(b p) d -> p b d", p=128)
    n = 0
    for i in range(4):
        sl = slice(i * 2048, (i + 1) * 2048)
        bs = slice(i * 16, (i + 1) * 16)
        for (kind, key) in (("k", ("ak", i)), ("v", ("av", i))):
            sg = stgq[n % 3]
            sk = "attstg%d" % (n % 3)
            n += 1
            if kind == "k":
                P.add("sp", lambda e: e.dma_start(out=sg[:], in_=kT_d[:, sl]), w=[sk], dma=True)
                P.add("dve", lambda e: e.tensor_copy(out=ks[:, sl], in_=sg[:]), r=[sk], w=[key])
            else:
                P.add("sp", lambda e: e.dma_start(out=sg[:].rearrange("p (b d) -> p b d", d=128), in_=vv[:, bs, :]), w=[sk], dma=True)
                P.add("act", lambda e: e.activation(out=vs[:, bs, :], in_=sg[:].rearrange("p (b d) -> p b d", d=128), func=AF.Copy), r=[sk], w=[key])
    scale = 1.0 / math.sqrt(128.0)
    LOOK = 2
    for qb in range(NQB):
        qsl = slice(qb * 512, (qb + 1) * 512)
        qs, qk_ = qblk[qb % 2], "attq%d" % (qb % 2)

        def qload(qq):
            P.add("sp", lambda e: e.dma_start(out=qst[qq % 2][:], in_=qT_d[:, qq * 512:(qq + 1) * 512]), w=["attqst%d" % (qq % 2)], dma=True)
            P.add("pool", lambda e: e.tensor_copy(out=qblk[qq % 2][:], in_=qst[qq % 2][:]), r=["attqst%d" % (qq % 2)], w=["attq%d" % (qq % 2)])

        if qb == 0:
            qload(0)
        if qb + 1 < NQB:
            qload(qb + 1)
        po, pko = C.psb[4 + qb % 2], "psb%d" % (4 + qb % 2)
        pl, pkl = C.psb[6 + qb % 2], "psb%d" % (6 + qb % 2)

        def smm(kb):
            ps, pks = C.psb[kb % 4], "psb%d" % (kb % 4)
            P.add("pe", lambda e, kb=kb, ps=ps: e.matmul(ps[:], lhsT=ks[:, kb * 128:(kb + 1) * 128], rhs=qs[:], start=True, stop=True),
                  r=[("ak", kb // 16), qk_], w=[pks])
            pt_, ptk = pT[kb % 4], "attp%d" % (kb % 4)
            P.add("act", lambda e, ps=ps, pt_=pt_: e.activation(out=pt_[:], in_=ps[:], func=AF.Exp, scale=scale), r=[pks], w=[ptk])

        for kb in range(min(LOOK, NKB)):
            smm(kb)
        for kb in range(NKB):
            if kb + LOOK < NKB:
                smm(kb + LOOK)
            pt_, ptk = pT[kb % 4], "attp%d" % (kb % 4)
            P.add("pe", lambda e, kb=kb, pt_=pt_: e.matmul(po[:], lhsT=vs[:, kb, :], rhs=pt_[:], start=(kb == 0), stop=(kb == NKB - 1)),
                  r=[("av", kb // 16), ptk], w=[pko])
            P.add("pe", lambda e, kb=kb, pt_=pt_: e.matmul(pl[:], lhsT=onesK[:], rhs=pt_[:], start=(kb == 0), stop=(kb == NKB - 1)),
                  r=["att_ones", ptk], w=[pkl])
        o_ = ob[qb % 2]
        ok = "atto%d" % (qb % 2)
        P.add("dve", lambda e: e.reciprocal(out=rl[:], in_=pl[:]), r=[pkl], w=["att_rl"])
        P.add("dve", lambda e, o_=o_: e.tensor_tensor(out=o_[:], in0=po[:], in1=rl[:], op=ALU.mult), r=[pko, "att_rl"], w=[ok])
        store(C, o_a[:, qsl], o_[:], [ok])
        if os.environ.get('DBG_BAR'):
            P.barrier()


def build_B(do_att=True, do_hy=True):
    C = new_ctx("B")
    P = C.P
    setup_common(C)
    cstB = din(C, "cstB", [128, 128])
    cstB_t = load_const(C, "cstB_sb", cstB, [128, 128], key="cstB")
    C.AR = Arena(P.sb("AR", [128, 23616], F32R), 23616)
    C.A32 = Arena(P.sb("A32", [128, 8704]), 8704)
    if do_att:
        qT = din(C, "qT", [128, L])
        kT = din(C, "kT", [128, L])
        v = din(C, "v", [L, 128])
        o_a = dout(C, "aT", [128, L])
        emit_attention(C, qT, kT, v, o_a, cstB_t)
        P.barrier()
    if do_att and do_hy:
        for nm, rows, cols in (("wo", 256, D), ("wu", 256, 2 * DFF), ("wd", DFF // NC, D), ("wi", 256, DIN)):
            src = din(C, nm + "_f32", [rows, cols])
            dst = dout(C, nm + "_bf", [rows, cols], BF16)
            nsp = 4 if nm == "wu" else 1
            rs = rows // nsp
            for i in range(nsp):
                C.uid += 1
                k = "out%d" % C.uid
                C.outkeys.append(k)
                P.add("pool", lambda e: e.dma_start(out=dst[i * rs:(i + 1) * rs, :], in_=src[i * rs:(i + 1) * rs, :]), w=[k], dma=True)
    if do_hy:
        emit_hyena(C)
    finish(C)
    return C


NFFT = 2 * L
HP_N = 18


def host_fftc():
    c = np.zeros((128, 1792), np.float64)
    hi = np.arange(64)[:, None]
    k1 = np.arange(128)[None, :]
    c[0:64, 0:128] = np.cos(2 * np.pi * hi * k1 / 128)
    c[0:64, 128:256] = -np.sin(2 * np.pi * hi * k1 / 128)
    p = np.arange(128)[:, None]
    f = np.arange(128)[None, :]
    c[:, 256:384] = np.cos(2 * np.pi * p * f / NFFT)
    c[:, 384:512] = np.sin(2 * np.pi * p * f / NFFT)
    C3 = np.cos(2 * np.pi * p * f / 128)
    S3 = np.sin(2 * np.pi * p * f / 128)
    c[:, 512:640] = C3
    c[:, 640:768] = S3
    c[:, 768:896] = -S3
    c[:, 896:960] = C3[:, 0:64] / NFFT
    c[:, 960:1024] = -S3[:, 0:64] / NFFT
    c[:, 1024:1152] = -S3
    c[:, 1152:1280] = C3
    c[:, 1280:1408] = -C3
    c[:, 1408:1536] = -C3
    c[:, 1536:1664] = -S3
    c[:, 1664:1728] = -C3[:, 0:64] / NFFT
    return c.astype(np.float32)


def host_zT():
    bands = 16
    t = np.linspace(0.0, 1.0, L, dtype=np.float32)[:, None]
    w = (2.0 * np.float32(math.pi) * np.arange(L, dtype=np.float32)[:, None] / np.float32(L)).astype(np.float32)
    f = np.linspace(1e-4, bands - 1, bands, dtype=np.float32)[None, :]
    z = np.concatenate([t, np.cos(f * w), -np.sin(f * w)], axis=-1).astype(np.float32)
    return np.ascontiguousarray(z.T)


def host_hy_inputs(inp, l, j, hyT):
    sl = slice(128 * j, 128 * j + 128)
    hu = np.concatenate([hyT[128 * j + 1024 * g: 128 * j + 1024 * g + 128] for g in range(3)], 0)
    hp = np.zeros((128, HP_N), np.float32)
    for g in range(3):
        for k in range(3):
            hp[:, g * 3 + k] = inp["hy_conv_w"][l][k, 1024 * g + 128 * j: 1024 * g + 128 * j + 128]
        hp[:, 9 + g] = inp["hy_conv_b"][l][1024 * g + 128 * j: 1024 * g + 128 * j + 128]
    hp[:, 12] = inp["filt_decay"][l][0, sl]
    hp[:, 13] = inp["filt_decay"][l][1, sl]
    hp[:, 14] = inp["hy_skip"][l][sl]
    hp[:, 16] = inp["filt_b3"][l][sl]
    hp[:, 17] = inp["filt_b3"][l][1024 + 128 * j: 1024 + 128 * j + 128]
    fw3 = np.concatenate([inp["filt_w3"][l][:, sl], inp["filt_w3"][l][:, 1024 + 128 * j: 1024 + 128 * j + 128]], 1)
    fpar = np.stack([inp["filt_b1"][l], inp["filt_f1"][l], inp["filt_b2"][l], inp["filt_f2"][l]], 1)
    tposm = np.tile(np.arange(512, dtype=np.float32)[None, :], (128, 1))
    blkv = np.tile((512.0 * np.arange(16, dtype=np.float32))[None, :], (128, 1))
    return {"hu": np.ascontiguousarray(hu), "hyp": hp, "fw1": np.ascontiguousarray(inp["filt_w1"][l]), "fw2": np.ascontiguousarray(inp["filt_w2"][l]),
            "fw3": np.ascontiguousarray(fw3), "fpar": np.ascontiguousarray(fpar), "zT": host_zT(), "tposm": tposm, "blkv": blkv, "fftc": host_fftc()}


def emit_hyena(C):
    import os
    P = C.P
    nc = C.nc
    hu = din(C, "hu", [384, L])
    hyp = din(C, "hyp", [128, HP_N])
    fw1 = din(C, "fw1", [33, 64])
    fw2 = din(C, "fw2", [64, 64])
    fw3 = din(C, "fw3", [64, 256])
    fpar = din(C, "fpar", [64, 4])
    zTd = din(C, "zT", [33, L])
    tposd = din(C, "tposm", [128, 512])
    blkd = din(C, "blkv", [128, 16])
    fftd = din(C, "fftc", [128, 1792])
    o_y = dout(C, "yT", [128, L])
    scr = {n: nc.dram_tensor("scr_" + n, [128, L], F32, kind="Internal").ap() for n in ("hf", "hb", "z", "y")}
    hp = load_const(C, "hyp_sb", hyp, [128, HP_N], key="hyp")
    w1t = load_const(C, "fw1_sb", fw1, [33, 64], key="fw")
    w2t = load_const(C, "fw2_sb", fw2, [64, 64], key="fw")
    w3t = load_const(C, "fw3_sb", fw3, [64, 256], key="fw")
    fpt = load_const(C, "fpar_sb", fpar, [64, 4], key="fpar")
    tpos = load_const(C, "tpos_sb", tposd, [128, 512], key="tpos")
    blkt = load_const(C, "blk_sb", blkd, [128, 16], key="blk")
    fft32 = load_const(C, "fft_sb", fftd, [128, 1792], key="fft32")
    AR, A32 = C.AR, C.A32
    AR.reset()
    A32.reset()
    fftr = AR.get([128, 1792])
    P.add("dve", lambda e: e.tensor_copy(out=fftr[:], in_=fft32[:]), r=["fft32"], w=["fftr"])
    tcr = P.sb("tcr", [128, 4, 128])
    tsr = P.sb("tsr", [128, 4, 128])
    for c in range(4):
        P.add("pool", lambda e: e.tensor_copy(out=tcr[:, c, :], in_=fft32[:, 256:384]), r=["fft32"], w=["tcr"])
        P.add("pool", lambda e: e.tensor_copy(out=tsr[:, c, :], in_=fft32[:, 384:512]), r=["fft32"], w=["tsr"])
    big1 = P.sb("hy_big1", [128, L])
    big2 = P.sb("hy_big2", [128, L])
    CH = 2048
    sgi = [0]

    def nstg():
        i = sgi[0] % 3
        sgi[0] += 1
        return stg[i], "hystg%d" % i

    par = P.sb("hy_par", [128, 48])
    P.add("dve", lambda e: e.tensor_scalar(out=par[0:64, 0:1], in0=fpt[:, 1:2], scalar1=1.0 / 3.0, scalar2=None, op0=ALU.mult), r=["fpar"], w=["par"])
    P.add("dve", lambda e: e.tensor_tensor(out=par[0:64, 1:2], in0=fpt[:, 0:1], in1=par[0:64, 0:1], op=ALU.mult), r=["fpar", "par"], w=["par"])
    P.add("dve", lambda e: e.tensor_scalar(out=par[0:64, 2:3], in0=fpt[:, 3:4], scalar1=1.0 / 3.0, scalar2=None, op0=ALU.mult), r=["fpar", "par"], w=["par"])
    P.add("dve", lambda e: e.tensor_tensor(out=par[0:64, 3:4], in0=fpt[:, 2:3], in1=par[0:64, 2:3], op=ALU.mult), r=["fpar", "par"], w=["par"])
    nsc = P.sb("hy_nsc", [128, 2])
    P.add("act", lambda e: e.activation(out=nsc[:], in_=hp[:, 12:14], func=AF.Abs), r=["hyp"], w=["nsc"])
    P.add("dve", lambda e: e.tensor_scalar(out=nsc[:], in0=nsc[:], scalar1=-1.0 / (L - 1), scalar2=None, op0=ALU.mult), r=["nsc"], w=["nsc"])
    P.add("dve", lambda e: e.tensor_copy(out=par[:, 4:6], in_=nsc[:]), r=["nsc", "par"], w=["par"])
    for d in range(2):
        P.add("dve", lambda e: e.tensor_scalar(out=par[:, 8 + 16 * d:24 + 16 * d], in0=blkt[:], scalar1=nsc[:, d:d + 1], scalar2=None, op0=ALU.mult),
              r=["blk", "par", "nsc"], w=["par"])
    if os.environ.get("DBG_STOP") == "1":
        return
    nacc = P.sb("hy_nacc", [128, 32])
    h1 = A32.get([64, CH])
    h2 = A32.get([64, CH])
    tmp = [A32.get([128, 512]) for i in range(4)]
    zt = A32.get([33, CH])
    for b4 in range(L // CH):
        P.add("sp", lambda e: e.dma_start(out=zt[:], in_=zTd[:, b4 * CH:(b4 + 1) * CH]), w=["zt"], dma=True)
        for (src, srck, wt, K_, dst, dstk, pc) in ((zt, "zt", w1t, 33, h1, "h1", 0), (h1, "h1", w2t, 64, h2, "h2", 2)):
            for s in range(CH // 512):
                ps, pk = C.psb[s % 4], "psb%d" % (s % 4)
                ss = slice(s * 512, (s + 1) * 512)
                P.add("pe", lambda e: e.matmul(ps[0:64, :], lhsT=wt[0:K_, :], rhs=src[0:K_, ss], start=True, stop=True), r=[srck if srck == "zt" else (srck, s), "fw"], w=[pk])
                a, ak = tmp[s % 2], "hytmp%d" % (s % 2)
                b, bk = tmp[2 + s % 2], "hytmp%d" % (2 + s % 2)
                P.add("act", lambda e: e.activation(out=a[0:64, :], in_=ps[0:64, :], func=AF.Sin, scale=par[0:64, pc:pc + 1], bias=par[0:64, pc + 1:pc + 2]),
                      r=[pk, "par"], w=[ak])
                P.add("dve", lambda e: e.tensor_tensor(out=b[0:64, :], in0=a[0:64, :], in1=a[0:64, :], op=ALU.mult), r=[ak], w=[bk])
                P.add("dve", lambda e: e.tensor_scalar(out=b[0:64, :], in0=b[0:64, :], scalar1=-4.0, scalar2=3.0, op0=ALU.mult, op1=ALU.add), r=[bk], w=[bk])
                P.add("dve", lambda e: e.tensor_tensor(out=dst[:, ss], in0=a[0:64, :], in1=b[0:64, :], op=ALU.mult), r=[ak, bk], w=[(dstk, s)])
        for d, big in ((0, big1), (1, big2)):
            for s in range(CH // 512):
                blk = b4 * 4 + s
                ps, pk = C.psb[4 + s % 4], "psb%d" % (4 + s % 4)
                ss = slice(s * 512, (s + 1) * 512)
                gs = slice(blk * 512, (blk + 1) * 512)
                P.add("pe", lambda e: e.matmul(ps[:], lhsT=w3t[:, d * 128:(d + 1) * 128], rhs=h2[:, ss], start=True, stop=True), r=[("h2", s), "fw"], w=[pk])
                a, ak = tmp[s % 2], "hytmp%d" % (s % 2)
                P.add("act", lambda e: e.activation(out=a[:], in_=tpos[:], func=AF.Exp, scale=par[:, 4 + d:5 + d], bias=par[:, 8 + 16 * d + blk:9 + 16 * d + blk]),
                      r=["tpos", "par"], w=[ak])
                P.add("dve", lambda e: e.scalar_tensor_tensor(out=big[:, gs], in0=ps[:], scalar=hp[:, 16 + d:17 + d], in1=a[:], op0=ALU.add, op1=ALU.mult),
                      r=[pk, ak, "hyp"], w=[("big%d" % (d + 1), blk // 4)])
                b, bk = tmp[2 + s % 2], "hytmp%d" % (2 + s % 2)
                P.add("act", lambda e: e.activation(out=b[:], in_=big[:, gs], func=AF.Abs, accum_out=nacc[:, d * 16 + blk:d * 16 + blk + 1]),
                      r=[("big%d" % (d + 1), blk // 4)], w=[bk, "nacc"])
    if os.environ.get("DBG_STOP") == "2":
        return
    P.add("dve", lambda e: e.tensor_reduce(out=par[:, 40:41], in_=nacc[:], axis=AX.X, op=ALU.add), r=["nacc", "par"], w=["par"])
    P.add("dve", lambda e: e.reciprocal(out=par[:, 41:42], in_=par[:, 40:41]), r=["par"], w=["par"])
    P.add("dve", lambda e: e.memset(big2[:, 0:1], 0.0), r=[("big2", 0)], w=[("big2", 0)])
    for q4 in range(4):
        qs_ = slice(q4 * CH, (q4 + 1) * CH)
        P.add("sp", lambda e: e.dma_start(out=scr["hf"][:, qs_], in_=big1[:, qs_]), r=[("big1", q4)], w=[("s_hf", q4)], dma=True)
        P.add("sp", lambda e: e.dma_start(out=scr["hb"][:, qs_], in_=big2[:, qs_]), r=[("big2", q4)], w=[("s_hb", q4)], dma=True)
    P.barrier()

    A32.reset()
    stg = [A32.get([128, CH + 2]) for i in range(3)]
    def conv_group(g, dst, dstkey):
        for q4 in range(4):
            sg, sk = nstg()
            lo = q4 * CH - 1
            hi_ = q4 * CH + CH + 1
            slo, shi = max(lo, 0), min(hi_, L)
            if q4 == 0:
                P.add("dve", lambda e: e.memset(sg[:, 0:1], 0.0), w=[sk])
            if q4 == 3:
                P.add("dve", lambda e: e.memset(sg[:, CH + 1:CH + 2], 0.0), w=[sk])
            P.add("sp", lambda e: e.dma_start(out=sg[:, slo - lo:slo - lo + (shi - slo)], in_=hu[g * 128:(g + 1) * 128, slo:shi]), r=[sk], w=[sk], dma=True)
            qs_ = slice(q4 * CH, (q4 + 1) * CH)
            dk = (dstkey, q4)
            P.add("act", lambda e: e.activation(out=dst[:, qs_], in_=sg[:, 1:CH + 1], func=AF.Identity, scale=hp[:, g * 3 + 1:g * 3 + 2], bias=hp[:, 9 + g:10 + g]),
                  r=[sk, "hyp"], w=[dk])
            P.add("dve", lambda e: e.scalar_tensor_tensor(out=dst[:, qs_], in0=sg[:, 0:CH], scalar=hp[:, g * 3:g * 3 + 1], in1=dst[:, qs_], op0=ALU.mult, op1=ALU.add),
                  r=[sk, dk, "hyp"], w=[dk])
            P.add("dve", lambda e: e.scalar_tensor_tensor(out=dst[:, qs_], in0=sg[:, 2:CH + 2], scalar=hp[:, g * 3 + 2:g * 3 + 3], in1=dst[:, qs_], op0=ALU.mult, op1=ALU.add),
                  r=[sk, dk, "hyp"], w=[dk])

    conv_group(1, big1, "big1")
    conv_group(2, big2, "big2")
    for q4 in range(4):
        qs_ = slice(q4 * CH, (q4 + 1) * CH)
        P.add("pool", lambda e: e.tensor_tensor(out=big1[:, qs_], in0=big1[:, qs_], in1=big2[:, qs_], op=ALU.mult), r=[("big1", q4), ("big2", q4)], w=[("big1", q4)])
        P.add("sp", lambda e: e.dma_start(out=scr["z"][:, qs_], in_=big1[:, qs_]), r=[("big1", q4)], w=[("s_z", q4)], dma=True)
    conv_group(0, big2, "big2")
    P.barrier()

    F1r = fftr[:, 0:256]
    C3r = fftr[:, 512:640]
    S3r = fftr[:, 640:768]
    nS3r = fftr[:, 768:896]
    nC3r = fftr[:, 1280:1408]
    C7r = fftr[:, 896:1024]
    nS7r = fftr[:, 960:1088]
    nC7r = fftr[:, 1664:1792]
    F5a = fftr[:, 512:768]
    F5b = fftr[:, 1024:1280]
    nF5a = fftr[:, 1408:1664]
    A32.reset()
    sin_ = [A32.get([64, 3, 4, 128]) for i in range(2)]
    absb = A32.get([128, 4, 2, 128])
    kre = A32.get([128, 4, 128])
    kim = A32.get([128, 4, 128])
    yst = [A32.get([64, 4, 128]) for i in range(2)]
    sr_ = [AR.get([128, 3, 4, 128]) for i in range(2)]
    for i in range(2):
        for j in range(3):
            P.add("dve", lambda e: e.tensor_scalar(out=sr_[i][64:128, j, :, :], in0=tcr[64:128, :, :], scalar1=0.0, scalar2=None, op0=ALU.mult),
                  r=["tcr"], w=["ffinr%d" % i])
    TS = [{(q, n): AR.get([128, 4, 128]) for q in "zfb" for n in range(4)} for i in range(2)]
    MM2 = [[AR.get([128, 4, 128]) for n in range(4)] for i in range(2)]
    UU = [AR.get([128, 4, 128]) for n in range(4)]
    psall = C.psall

    def pv(b0):
        return psall[:, b0 * 512:(b0 + 2) * 512].rearrange("p (c r k) -> p c r k", c=4, r=2), ["psb%d" % b0, "psb%d" % (b0 + 1)]

    def prods(eng, dst, dkeys, are, aim, akeys):
        for n, (a, t, tk) in enumerate(((are, tcr, "tcr"), (aim, tsr, "tsr"), (aim, tcr, "tcr"), (are, tsr, "tsr"))):
            P.add(eng, lambda e: e.tensor_tensor(out=dst[n][:], in0=a, in1=t[:], op=ALU.mult), r=akeys + [tk], w=[dkeys[n]])

    def s1(g, i, b0):
        sr, srk = sr_[g % 2], "ffinr%d" % (g % 2)
        for c in range(4):
            b = b0 + c // 2
            P.add("pe", lambda e: e.matmul(psall[:, b * 512 + (c % 2) * 256: b * 512 + (c % 2) * 256 + 256], lhsT=sr[:, i, c, :], rhs=F1r, start=True, stop=True),
                  r=[srk, "fftr"], w=["psb%d" % b])

    def acc(bank, terms, npart=128):
        n = len(terms)
        for i, (l, rt, rk_) in enumerate(terms):
            P.add("pe", lambda e: e.matmul(C.psb[bank][0:npart, :], lhsT=l, rhs=rt[:].rearrange("p c k -> p (c k)"), start=(i == 0), stop=(i == n - 1)),
                  r=[rk_, "fftr"], w=["psb%d" % bank])

    NG = int(os.environ.get("DBG_NG", 32))

    def ffload(gg):
        c0 = 4 * gg
        si, sik = sin_[gg % 2], "ffin%d" % (gg % 2)
        sr, srk = sr_[gg % 2], "ffinr%d" % (gg % 2)
        for i, n in enumerate(("z", "hf", "hb")):
            P.add("sp", lambda e: e.dma_start(out=si[:, i, :, :], in_=scr[n][c0:c0 + 4, :].rearrange("c (h l) -> h c l", l=128)),
                  r=[("s_" + n, q) for q in range(4)], w=[(sik, i)], dma=True)
        P.add("act", lambda e: e.activation(out=sr[0:64], in_=si[:], func=AF.Copy), r=[(sik, i) for i in range(3)], w=[srk])

    for g in range(NG + 3):
        cur = g < NG
        p1 = g - 1
        p2 = g - 2
        p3 = g - 3
        has1 = 1 <= g <= NG
        has2 = 2 <= g <= NG + 1
        has3 = 3 <= g <= NG + 2
        if has3:
            acc(4, [(C7r, UU[0], "UU0"), (nC7r, UU[1], "UU1"), (nS7r, UU[2], "UU2"), (nS7r, UU[3], "UU3")], npart=128)
            ys, ysk = yst[p3 % 2], "ffyst%d" % (p3 % 2)
            pc0 = 4 * p3
            P.add("act", lambda e: e.activation(out=ys[:], in_=C.psb[4][0:64, :].rearrange("p (c k) -> p c k", c=4), func=AF.Copy), r=["psb4"], w=[ysk])
            P.add("pool", lambda e: e.dma_start(out=scr["y"][pc0:pc0 + 4, :].rearrange("c (h l) -> h c l", l=128), in_=ys[:]), r=[ysk], w=[("s_y", p3)], dma=True)
        if has2:
            M2 = MM2[p2 % 2]
            for c in range(4):
                b = 6 + c // 2
                o_ap = psall[:, b * 512 + (c % 2) * 256: b * 512 + (c % 2) * 256 + 256]
                for i, (mi, tab) in enumerate(((0, F5a), (1, nF5a), (2, F5b), (3, F5b))):
                    P.add("pe", lambda e: e.matmul(o_ap, lhsT=M2[mi][:, c, :], rhs=tab, start=(i == 0), stop=(i == 3)), r=["MM%d_%d" % (p2 % 2, mi), "fftr"], w=["psb%d" % b])
            dd, ddk = pv(6)
            prods("dve", UU, ["UU%d" % n for n in range(4)], dd[:, :, 0, :], dd[:, :, 1, :], ddk)
        if cur:
            T_ = TS[g % 2]
            tk_ = lambda q, n: "T%d%s%d" % (g % 2, q, n)
            if g == 0:
                ffload(0)
            s1(g, 0, 0)
            s1(g, 2, 2)
            az, azk = pv(0)
            prods("dve", [T_[("z", n)] for n in range(4)], [tk_("z", n) for n in range(4)], az[:, :, 0, :], az[:, :, 1, :], azk)
            ab, abk = pv(2)
            P.add("act", lambda e: e.activation(out=absb[:], in_=ab, func=AF.Copy), r=abk, w=["absb"])
        if g + 1 < NG:
            ffload(g + 1)
        if has1:
            Tp = TS[p1 % 2]
            pk_ = lambda q, n: "T%d%s%d" % (p1 % 2, q, n)
            tz = lambda n: (Tp[("z", n)], pk_("z", n))
            tf = lambda n: (Tp[("f", n)], pk_("f", n))
            tb = lambda n: (Tp[("b", n)], pk_("b", n))
            acc(4, [(C3r,) + tz(0), (C3r,) + tz(1), (S3r,) + tz(2), (nS3r,) + tz(3)])
            acc(5, [(C3r,) + tz(2), (nC3r,) + tz(3), (nS3r,) + tz(0), (nS3r,) + tz(1)])
            acc(6, [(C3r,) + tf(0), (C3r,) + tf(1), (S3r,) + tf(2), (nS3r,) + tf(3), (C3r,) + tb(0), (C3r,) + tb(1), (S3r,) + tb(2), (nS3r,) + tb(3)])
            acc(7, [(C3r,) + tf(2), (nC3r,) + tf(3), (nS3r,) + tf(0), (nS3r,) + tf(1), (nC3r,) + tb(2), (C3r,) + tb(3), (S3r,) + tb(0), (S3r,) + tb(1)])
            P.add("act", lambda e: e.activation(out=kre[:], in_=C.psb[6][:].rearrange("p (c k) -> p c k", c=4), func=AF.Copy), r=["psb6"], w=["kre"])
            P.add("act", lambda e: e.activation(out=kim[:], in_=C.psb[7][:].rearrange("p (c k) -> p c k", c=4), func=AF.Copy), r=["psb7"], w=["kim"])
        if cur:
            s1(g, 1, 0)
            prods("pool", [T_[("b", n)] for n in range(4)], [tk_("b", n) for n in range(4)], absb[:, :, 0, :], absb[:, :, 1, :], ["absb"])
        if has1:
            xre = C.psb[4][:].rearrange("p (c k) -> p c k", c=4)
            xim = C.psb[5][:].rearrange("p (c k) -> p c k", c=4)
            M1 = MM2[p1 % 2]
            for n, (a, ak, k_, kk) in enumerate(((xre, "psb4", kre, "kre"), (xim, "psb5", kim, "kim"), (xre, "psb4", kim, "kim"), (xim, "psb5", kre, "kre"))):
                P.add("dve", lambda e: e.tensor_tensor(out=M1[n][:], in0=a, in1=k_[:], op=ALU.mult), r=[ak, kk], w=["MM%d_%d" % (p1 % 2, n)])
        if cur:
            af, afk = pv(0)
            prods("dve", [T_[("f", n)] for n in range(4)], [tk_("f", n) for n in range(4)], af[:, :, 0, :], af[:, :, 1, :], afk)
    P.barrier()

    A32.reset()
    stg = [A32.get([128, CH + 2]) for i in range(3)]
    for q4 in range(4):
        qs_ = slice(q4 * CH, (q4 + 1) * CH)
        sg, sk = nstg()
        P.add("sp", lambda e: e.dma_start(out=sg[:, 0:CH], in_=scr["y"][:, qs_]), w=[sk], dma=True)
        P.add("act", lambda e: e.activation(out=sg[:, 0:CH], in_=sg[:, 0:CH], func=AF.Copy, scale=par[:, 41:42]), r=[sk, "par"], w=[sk])
        P.add("dve", lambda e: e.scalar_tensor_tensor(out=sg[:, 0:CH], in0=big1[:, qs_], scalar=hp[:, 14:15], in1=sg[:, 0:CH], op0=ALU.mult, op1=ALU.add),
              r=[sk, "hyp"], w=[sk])
        P.add("pool", lambda e: e.tensor_tensor(out=sg[:, 0:CH], in0=sg[:, 0:CH], in1=big2[:, qs_], op=ALU.mult), r=[sk], w=[sk])
        store(C, o_y[:, qs_], sg[:, 0:CH], [sk])


_PROGS = {}


def _prog(name):
    if name not in _PROGS:
        if name == "A0":
            _PROGS[name] = build_A0()
        elif name == "B":
            _PROGS[name] = build_B(True, True)
        elif name == "C1":
            _PROGS[name] = build_C(True)
        elif name == "C0":
            _PROGS[name] = build_C(False)
    return _PROGS[name]


def _run(name, maps):
    C = _prog(name)
    res = run_bass_kernel_spmd(C.nc, maps, core_ids=list(range(NC)))
    return res.results


def _win(mT, core):
    o = np.zeros((mT.shape[0], TPC + 2), np.float32)
    lo = core * TPC - 1
    hi = core * TPC + TPC + 1
    slo, shi = max(lo, 0), min(hi, L)
    o[:, slo - lo: slo - lo + (shi - slo)] = mT[:, slo:shi]
    return o


def _f16(v):
    return np.ascontiguousarray(np.asarray(v, np.float32).reshape(16, 128).T)


def kernel(**inp):
    inp = {k: np.asarray(v) for k, v in inp.items()}
    cos, sin = host_rope()
    cstA = host_cstA()
    cstC = np.full((128, 128), 1.0 / 1024, np.float32)
    cstB = np.ones((128, 128), np.float32)
    xT = np.ascontiguousarray(inp["x"][0].T)

    def qkg(l):
        return np.ascontiguousarray(np.stack([inp["q_norm_g"][l], inp["k_norm_g"][l]], 1).astype(np.float32))

    def tsl(i):
        return slice(i * TPC, (i + 1) * TPC)

    maps = []
    for i in range(NC):
        maps.append({"xT": np.ascontiguousarray(xT[:, tsl(i)]), "w_in": inp["w_in"][0], "lng": _f16(inp["ln_in_g"]), "lnb": _f16(inp["ln_in_b"]),
                     "qkg": qkg(0), "cos": np.ascontiguousarray(cos[:, tsl(i)]), "sin": np.ascontiguousarray(sin[:, tsl(i)]), "cstA": cstA})
    r = _run("A0", maps)
    cat = lambda key: np.concatenate([r[i][key] for i in range(NC)], axis=1)
    qT, kT, hyT, xcur = cat("qT"), cat("kT"), cat("hyT"), cat("x0T")
    v = np.concatenate([r[i]["v"] for i in range(NC)], axis=0)
    out = None
    for l in range(2):
        maps = []
        for j in range(NC):
            kv = j // 4
            m = {"cstB": cstB, "qT": np.ascontiguousarray(qT[j * 128:(j + 1) * 128]), "kT": np.ascontiguousarray(kT[kv * 128:(kv + 1) * 128]),
                 "v": np.ascontiguousarray(v[:, kv * 128:(kv + 1) * 128])}
            m.update(host_hy_inputs(inp, l, j, hyT))
            rw = DFF // NC
            m.update({"wo_f32": np.ascontiguousarray(inp["w_out"][l][j * 256:(j + 1) * 256]), "wu_f32": np.ascontiguousarray(inp["w_up"][l][j * 256:(j + 1) * 256]),
                      "wd_f32": np.ascontiguousarray(inp["w_down"][l][j * rw:(j + 1) * rw]), "wi_f32": np.ascontiguousarray(inp["w_in"][1][j * 256:(j + 1) * 256])})
            maps.append(m)
        r = _run("B", maps)
        wbf = {nm: np.concatenate([r[j][nm + "_bf"] for j in range(NC)], axis=0) for nm in ("wo", "wu", "wd", "wi")}
        aT = np.concatenate([r[j]["aT"] for j in range(NC)], axis=0)
        yT = np.concatenate([r[j]["yT"] for j in range(NC)], axis=0)
        maps = []
        for i in range(NC):
            m = {"aT": _win(aT, i), "yT": _win(yT, i), "xT": _win(xcur, i), "w_out": wbf["wo"], "w_up": wbf["wu"], "w_down": wbf["wd"],
                 "prmC": host_prmC(inp, l, i), "cstA": cstA, "cstC": cstC}
            if l == 0:
                m.update({"w_in": wbf["wi"], "qkg": qkg(1), "cos": np.ascontiguousarray(cos[:, tsl(i)]), "sin": np.ascontiguousarray(sin[:, tsl(i)])})
            maps.append(m)
        r = _run("C1" if l == 0 else "C0", maps)
        xcur = np.concatenate([r[i]["x2T"] for i in range(NC)], axis=1)
        if l == 0:
            qT, kT, hyT = cat("qT"), cat("kT"), cat("hyT")
            v = np.concatenate([r[i]["v"] for i in range(NC)], axis=0)
    out = np.ascontiguousarray(xcur.T)[None].astype(np.float32)
    return out
```
